# Optimizing a Trainium2 kernel written in Bass

```python
import jax, jax.numpy as jnp
from jax import lax
import numpy as np

D_MODEL = 1024
BATCH = 8
SEQ = 2048
DEPTH = 1

CTX_LEN = 256
GRID_W = 64
EPS = 1e-6

ATT_HEADS = 8
ATT_KV_HEADS = 2
ATT_GROUP = ATT_HEADS // ATT_KV_HEADS
ATT_HEAD_DIM = 64
WINDOW = 128
ATT_BLOCK = 128
ROPE_BASE = 10000.0
ROPE_AXIS_DIM = ATT_HEAD_DIM // 2

ML_HEADS = 4
ML_V_DIM = 128
ML_QK_DIM = 64
ML_CHUNK = 64
ML_CONV = 3

ATT_WIDTH = ATT_HEADS * ATT_HEAD_DIM
ATT_KV_WIDTH = ATT_KV_HEADS * ATT_HEAD_DIM
ML_WIDTH = ML_HEADS * ML_V_DIM
ML_QK_WIDTH = ML_HEADS * ML_QK_DIM
ML_GATES = 4 * ML_HEADS
MIX_WIDTH = ATT_WIDTH + ML_WIDTH
FFN_HIDDEN = -(-8 * D_MODEL // (3 * 256)) * 256

COLS = (ATT_WIDTH, ATT_KV_WIDTH, ATT_KV_WIDTH, ML_QK_WIDTH, ML_QK_WIDTH, ML_WIDTH, ML_WIDTH, ML_GATES)
IN_WIDTH = sum(COLS)
SPLIT_AT = tuple(sum(COLS[:i + 1]) for i in range(len(COLS) - 1))

kernel_name = 'hymba_style_mlstm_swa_sandwich_adaln_block'


def rms_norm(x, g):
    xf = x.astype(jnp.float32)
    y = xf * lax.rsqrt(jnp.mean(xf * xf, axis=-1, keepdims=True) + EPS)
    return (y * g.astype(jnp.float32)).astype(x.dtype)


def modulate(h, shift, scale):
    return h * (1.0 + scale) + shift


def axial_rope_tables(n_tokens, dtype):
    n_rows = n_tokens // GRID_W
    row = jnp.repeat(jnp.arange(n_rows, dtype=jnp.float32), GRID_W)
    col = jnp.tile(jnp.arange(GRID_W, dtype=jnp.float32), n_rows)
    half = ROPE_AXIS_DIM // 2
    inv_freq = jnp.power(ROPE_BASE, -jnp.arange(half, dtype=jnp.float32) / half)
    ang_r = row[:, None] * inv_freq
    ang_c = col[:, None] * inv_freq
    return tuple(t[:, None, :].astype(dtype) for t in (jnp.cos(ang_r), jnp.sin(ang_r), jnp.cos(ang_c), jnp.sin(ang_c)))


def axial_rope(x, cos_r, sin_r, cos_c, sin_c):
    def rot(u, cos, sin):
        u1, u2 = jnp.split(u, 2, axis=-1)
        return jnp.concatenate([u1 * cos - u2 * sin, u2 * cos + u1 * sin], axis=-1)
    xr, xc = jnp.split(x, 2, axis=-1)
    return jnp.concatenate([rot(xr, cos_r, sin_r), rot(xc, cos_c, sin_c)], axis=-1)


def window_attention(q, k, v, kc, vc, sink):
    B, S = q.shape[:2]
    nb = S // ATT_BLOCK
    scale = ATT_HEAD_DIM ** -0.5
    qb = q.reshape(B, nb, ATT_BLOCK, ATT_KV_HEADS, ATT_GROUP, ATT_HEAD_DIM)
    pad = ((0, 0), (ATT_BLOCK, ATT_BLOCK), (0, 0), (0, 0))

    def band(t):
        tp = jnp.pad(t, pad).reshape(B, nb + 2, ATT_BLOCK, ATT_KV_HEADS, ATT_HEAD_DIM)
        return jnp.concatenate([tp[:, :-2], tp[:, 1:-1], tp[:, 2:]], axis=2)

    kw, vw = band(k), band(v)
    qpos = jnp.arange(S).reshape(nb, ATT_BLOCK)
    kpos = (jnp.arange(nb)[:, None] - 1) * ATT_BLOCK + jnp.arange(3 * ATT_BLOCK)[None, :]
    valid = ((jnp.abs(qpos[:, :, None] - kpos[:, None, :]) <= WINDOW)
             & (kpos[:, None, :] >= 0) & (kpos[:, None, :] < S))
    s_loc = jnp.einsum('bnqhgd,bnkhd->bnhgqk', qb, kw).astype(jnp.float32) * scale
    s_loc = jnp.where(valid[None, :, None, None], s_loc, -jnp.inf)
    s_ctx = jnp.einsum('bnqhgd,bchd->bnhgqc', qb, kc).astype(jnp.float32) * scale
    sk = sink.astype(jnp.float32).reshape(1, 1, ATT_KV_HEADS, ATT_GROUP, 1, 1)
    m = jnp.maximum(jnp.maximum(s_loc.max(-1, keepdims=True), s_ctx.max(-1, keepdims=True)), sk)
    p_loc = jnp.exp(s_loc - m)
    p_ctx = jnp.exp(s_ctx - m)
    inv = 1.0 / (p_loc.sum(-1, keepdims=True) + p_ctx.sum(-1, keepdims=True) + jnp.exp(sk - m))
    o = (jnp.einsum('bnhgqk,bnkhd->bnqhgd', (p_loc * inv).astype(v.dtype), vw)
         + jnp.einsum('bnhgqc,bchd->bnqhgd', (p_ctx * inv).astype(v.dtype), vc))
    return o.reshape(B, S, ATT_WIDTH)


def context_self_attention(qc, kc, vc, sink):
    B, L = qc.shape[:2]
    scale = ATT_HEAD_DIM ** -0.5
    qg = qc.reshape(B, L, ATT_KV_HEADS, ATT_GROUP, ATT_HEAD_DIM)
    s = jnp.einsum('bqhgd,bkhd->bhgqk', qg, kc).astype(jnp.float32) * scale
    sk = sink.astype(jnp.float32).reshape(1, ATT_KV_HEADS, ATT_GROUP, 1, 1)
    m = jnp.maximum(s.max(-1, keepdims=True), sk)
    p = jnp.exp(s - m)
    p = p / (p.sum(-1, keepdims=True) + jnp.exp(sk - m))
    o = jnp.einsum('bhgqk,bkhd->bqhgd', p.astype(vc.dtype), vc)
    return o.reshape(B, L, ATT_WIDTH)


def centred_dwconv(t, w):
    pad = w.shape[0] // 2
    return lax.conv_general_dilated(t, w[:, None, :].astype(t.dtype), window_strides=(1,),
                                    padding=[(pad, pad)], dimension_numbers=('NWC', 'WIO', 'NWC'),
                                    feature_group_count=t.shape[-1])


def mlstm_heads(t, d):
    B, T = t.shape[:2]
    return t.reshape(B, T, ML_HEADS, d).transpose(0, 2, 1, 3).astype(jnp.float32)


def mlstm_prepare(mq, mk, mv, mg, w_conv, gate_bias):
    qk = jax.nn.silu(centred_dwconv(jnp.concatenate([mq, mk], axis=-1), w_conv))
    mq, mk = jnp.split(qk, 2, axis=-1)
    B, T = mq.shape[:2]
    g = (mg.astype(jnp.float32) + gate_bias.astype(jnp.float32)).reshape(B, T, 4, ML_HEADS).transpose(0, 3, 1, 2)
    return mlstm_heads(mq, ML_QK_DIM), mlstm_heads(mk, ML_QK_DIM), mlstm_heads(mv, ML_V_DIM), g


def mlstm_zero_state(B):
    return (jnp.zeros((B, ML_HEADS, ML_V_DIM, ML_QK_DIM), jnp.float32),
            jnp.zeros((B, ML_HEADS, ML_QK_DIM), jnp.float32),
            jnp.zeros((B, ML_HEADS), jnp.float32))


def mlstm_chunkwise(q, k, v, ig, lf, state):
    B, H, T, _ = q.shape
    nc = T // ML_CHUNK
    q = q * (ML_QK_DIM ** -0.5)

    def to_chunks(a):
        return jnp.moveaxis(a.reshape(B, H, nc, ML_CHUNK, *a.shape[3:]), 2, 0)

    xs = tuple(to_chunks(a) for a in (q, k, v, ig, lf))
    lower = jnp.tril(jnp.ones((ML_CHUNK, ML_CHUNK), dtype=bool))

    def step(carry, inp):
        C, n, m = carry
        qc, kc, vc, ic, fc = inp
        b = jnp.cumsum(fc, axis=-1)
        a_inter = b + m[..., None]
        d = jnp.where(lower, b[..., :, None] - b[..., None, :] + ic[..., None, :], -jnp.inf)
        m_t = jnp.maximum(a_inter, d.max(-1))
        w_intra = jnp.exp(d - m_t[..., None])
        w_inter = jnp.exp(a_inter - m_t)
        s = jnp.einsum('bhtd,bhsd->bhts', qc, kc) * w_intra
        num = jnp.einsum('bhts,bhsv->bhtv', s, vc) + w_inter[..., None] * jnp.einsum('bhvd,bhtd->bhtv', C, qc)
        den = s.sum(-1) + w_inter * jnp.einsum('bhd,bhtd->bht', n, qc)
        h = num / jnp.maximum(jnp.abs(den), jnp.exp(-m_t))[..., None]
        b_last = b[..., -1]
        g = b_last[..., None] - b + ic
        m_new = jnp.maximum(b_last + m, g.max(-1))
        decay = jnp.exp(b_last + m - m_new)
        wk = jnp.exp(g - m_new[..., None])
        C = decay[..., None, None] * C + jnp.einsum('bhs,bhsv,bhsd->bhvd', wk, vc, kc)
        n = decay[..., None] * n + jnp.einsum('bhs,bhsd->bhd', wk, kc)
        return (C, n, m_new), h

    state, hs = lax.scan(step, state, xs)
    return jnp.moveaxis(hs, 0, 2).reshape(B, H, T, ML_V_DIM), state


def mlstm_bidir(q, k, v, g, state_f, state_b):
    h_f, st_f = mlstm_chunkwise(q, k, v, g[..., 0], jax.nn.log_sigmoid(g[..., 1]), state_f)
    flip = lambda t: jnp.flip(t, axis=2)
    h_b, st_b = mlstm_chunkwise(flip(q), flip(k), flip(v), flip(g[..., 2]),
                                flip(jax.nn.log_sigmoid(g[..., 3])), state_b)
    return h_f + flip(h_b), st_f, st_b


def mlstm_output(h, o_pre, gain):
    B, H, T, DV = h.shape
    hf = h.transpose(0, 2, 1, 3)
    hf = hf * lax.rsqrt(jnp.mean(hf * hf, axis=-1, keepdims=True) + EPS) * gain.astype(jnp.float32).reshape(ML_HEADS, ML_V_DIM)
    return hf.reshape(B, T, ML_WIDTH).astype(o_pre.dtype) * jax.nn.sigmoid(o_pre)


def swiglu(h, w_in, w_out):
    gate, up = jnp.split(h @ w_in, 2, axis=-1)
    return (jax.nn.silu(gate) * up) @ w_out


def setup_inputs(seed: int = 0) -> dict:
    key = jax.random.key(seed)
    ks = jax.random.split(key, 20)
    nrm = lambda k, shape, s: jax.random.normal(k, shape, jnp.float32) * s
    fgt = jnp.linspace(3.0, 6.0, ML_HEADS, dtype=jnp.float32)
    zh = jnp.zeros((ML_HEADS,), jnp.float32)
    gate_base = jnp.concatenate([zh, fgt, zh, fgt])
    return {
        'x': nrm(ks[0], (BATCH, SEQ, D_MODEL), 1.0),
        'c': nrm(ks[1], (BATCH, D_MODEL), 1.0),
        'ctx': nrm(ks[2], (BATCH, CTX_LEN, D_MODEL), 1.0),
        'c_ctx': nrm(ks[3], (D_MODEL,), 1.0),
        'w_ada': nrm(ks[4], (DEPTH, D_MODEL, 6 * D_MODEL), 0.5 * D_MODEL ** -0.5),
        'b_ada': nrm(ks[5], (DEPTH, 6 * D_MODEL), 0.02),
        'g_pre_mix': 1.0 + nrm(ks[6], (DEPTH, D_MODEL), 0.05),
        'w_in': nrm(ks[7], (DEPTH, D_MODEL, IN_WIDTH), D_MODEL ** -0.5),
        'w_conv_qk': nrm(ks[8], (DEPTH, ML_CONV, 2 * ML_QK_WIDTH), ML_CONV ** -0.5),
        'b_gates': gate_base[None, :] + nrm(ks[9], (DEPTH, ML_GATES), 0.1),
        'attn_sink': nrm(ks[10], (DEPTH, ATT_HEADS), 0.5),
        'g_mlstm_out': 1.0 + nrm(ks[11], (DEPTH, ML_WIDTH), 0.05),
        'w_out': nrm(ks[12], (DEPTH, MIX_WIDTH, D_MODEL), MIX_WIDTH ** -0.5),
        'g_post_mix': 1.0 + nrm(ks[13], (DEPTH, D_MODEL), 0.05),
        'g_pre_ffn': 1.0 + nrm(ks[14], (DEPTH, D_MODEL), 0.05),
        'w_ffn_in': nrm(ks[15], (DEPTH, D_MODEL, 2 * FFN_HIDDEN), D_MODEL ** -0.5),
        'w_ffn_out': nrm(ks[16], (DEPTH, FFN_HIDDEN, D_MODEL), FFN_HIDDEN ** -0.5),
        'g_post_ffn': 1.0 + nrm(ks[17], (DEPTH, D_MODEL), 0.05),
    }


def reference(x, c, ctx, c_ctx, w_ada, b_ada, g_pre_mix, w_in, w_conv_qk, b_gates, attn_sink,
              g_mlstm_out, w_out, g_post_mix, g_pre_ffn, w_ffn_in, w_ffn_out, g_post_ffn):
    B, S, _ = x.shape
    L = ctx.shape[1]
    rope = axial_rope_tables(S, x.dtype)
    silu_c = jax.nn.silu(c)
    silu_cc = jax.nn.silu(c_ctx)
    for layer in range(DEPTH):
        advance_ctx = layer + 1 < DEPTH
        mod_x = (silu_c @ w_ada[layer] + b_ada[layer])[:, None, :]
        mod_c = silu_cc @ w_ada[layer] + b_ada[layer]
        sh_m, sc_m, gt_m, sh_f, sc_f, gt_f = jnp.split(mod_x, 6, axis=-1)
        csh_m, csc_m, cgt_m, csh_f, csc_f, cgt_f = jnp.split(mod_c, 6, axis=-1)

        px = modulate(rms_norm(x, g_pre_mix[layer]), sh_m, sc_m) @ w_in[layer]
        pc = modulate(rms_norm(ctx, g_pre_mix[layer]), csh_m, csc_m) @ w_in[layer]
        aqx, akx, avx, mqx, mkx, mvx, mox, mgx = jnp.split(px, SPLIT_AT, axis=-1)
        aqc, akc, avc, mqc, mkc, mvc, moc, mgc = jnp.split(pc, SPLIT_AT, axis=-1)

        kc_att = akc.reshape(B, L, ATT_KV_HEADS, ATT_HEAD_DIM)
        vc_att = avc.reshape(B, L, ATT_KV_HEADS, ATT_HEAD_DIM)
        qx_att = axial_rope(aqx.reshape(B, S, ATT_HEADS, ATT_HEAD_DIM), *rope)
        kx_att = axial_rope(akx.reshape(B, S, ATT_KV_HEADS, ATT_HEAD_DIM), *rope)
        att_x = window_attention(qx_att, kx_att, avx.reshape(B, S, ATT_KV_HEADS, ATT_HEAD_DIM),
                                 kc_att, vc_att, attn_sink[layer])

        cq, ck, cv, cg = mlstm_prepare(mqc, mkc, mvc, mgc, w_conv_qk[layer], b_gates[layer])
        h_ml_c, st_f, st_b = mlstm_bidir(cq, ck, cv, cg, mlstm_zero_state(B), mlstm_zero_state(B))
        xq, xk, xv, xg = mlstm_prepare(mqx, mkx, mvx, mgx, w_conv_qk[layer], b_gates[layer])
        h_ml_x, _, _ = mlstm_bidir(xq, xk, xv, xg, st_f, st_b)
        ml_x = mlstm_output(h_ml_x, mox, g_mlstm_out[layer])

        mix_x = jnp.concatenate([att_x, ml_x], axis=-1) @ w_out[layer]
        x = x + gt_m * rms_norm(mix_x, g_post_mix[layer])
        if advance_ctx:
            att_c = context_self_attention(aqc.reshape(B, L, ATT_HEADS, ATT_HEAD_DIM), kc_att, vc_att, attn_sink[layer])
            ml_c = mlstm_output(h_ml_c, moc, g_mlstm_out[layer])
            mix_c = jnp.concatenate([att_c, ml_c], axis=-1) @ w_out[layer]
            ctx = ctx + cgt_m * rms_norm(mix_c, g_post_mix[layer])

        fx = swiglu(modulate(rms_norm(x, g_pre_ffn[layer]), sh_f, sc_f), w_ffn_in[layer], w_ffn_out[layer])
        x = x + gt_f * rms_norm(fx, g_post_ffn[layer])
        if advance_ctx:
            fc = swiglu(modulate(rms_norm(ctx, g_pre_ffn[layer]), csh_f, csc_f), w_ffn_in[layer], w_ffn_out[layer])
            ctx = ctx + cgt_f * rms_norm(fc, g_post_ffn[layer])
    return x
```

```python
import os
import numpy as np
import ml_dtypes
from contextlib import ExitStack
import concourse.bass as bass
import concourse.mybir as mybir
from concourse.bass_utils import run_bass_kernel_spmd

F32 = mybir.dt.float32
BF16 = mybir.dt.bfloat16
ALU = mybir.AluOpType
AF = mybir.ActivationFunctionType

D = 1024
S_LEN = 2048
L_CTX = 256
NT = 18
NX = 16
FH = 2816
NCH = 22
EPS = 1e-6
NEG = -30000.0
DEBUG = bool(int(os.environ.get('MK_DEBUG', '0')))
STOP = float(os.environ.get('MK_STOP', '99'))
SUB = int(os.environ.get('MK_SUB', '9'))
STRICT = bool(int(os.environ.get('MK_STRICT', '1')))


class StopBuild(Exception):
    pass


class T:
    def __init__(self, name=""):
        self.name = name
        self.w = None
        self.r = []


class Op:
    __slots__ = ("eng", "idx", "fn", "deps", "raw", "inc", "cnt", "dma", "dsem", "dval", "gidx", "cost", "nbytes", "seg", "fin")
    _G = [0]

    def __init__(self, eng, idx, fn, dma):
        Op._G[0] += 1
        self.gidx = Op._G[0]
        self.eng = eng; self.idx = idx; self.fn = fn; self.deps = set(); self.raw = set(); self.inc = False
        self.cnt = 0; self.dma = dma; self.dsem = None; self.dval = 0
        self.cost = 0.1; self.nbytes = 0; self.seg = 0; self.fin = 0.0


class Sched:
    ENG = ["pe", "act", "dve", "pool", "sp"]
    NDMA = 8

    def __init__(self):
        self.ops = {e: [] for e in self.ENG}
        self.bar = []
        self.seg = 0

    def add(self, eng, fn, reads=(), writes=(), dma=False, cost=0.1, nbytes=0):
        op = Op(eng, len(self.ops[eng]), fn, dma)
        op.cost = cost; op.nbytes = nbytes; op.seg = self.seg
        for t in reads:
            if t.w is not None:
                op.deps.add(t.w)
                op.raw.add(t.w)
        for t in writes:
            if t.w is not None:
                op.deps.add(t.w)
            for r in t.r:
                op.deps.add(r)
        for t in reads:
            if not dma:
                keep = []
                for r in t.r:
                    if r.dma or r.eng != eng:
                        keep.append(r)
                    else:
                        op.deps.add(r)
                t.r = keep
            t.r.append(op)
        for t in writes:
            t.w = op
            t.r = []
        op.deps.discard(op)
        self.ops[eng].append(op)
        return op

    def barrier(self):
        self.seg += 1

    def schedule(self):
        BW = 200e3
        final = {e: [] for e in self.ENG}
        nseg = self.seg + 1
        free = {e: 0.0 for e in self.ENG}
        dma_free = 0.0
        prev_last = []
        W = int(os.environ.get("MK_W", "40"))
        FIXED = set(os.environ.get("MK_FIX", "none").split(","))
        for sg in range(nseg):
            lists = {e: [o for o in self.ops[e] if o.seg == sg] for e in self.ENG}
            ptr = {e: 0 for e in self.ENG}
            done = set()
            segstart = max(free.values()) if sg > 0 else 0.0
            for e in self.ENG:
                free[e] = max(free[e], segstart)
            remaining = sum(len(v) for v in lists.values())
            sched_flag = {}
            while remaining:
                best = None
                for e in self.ENG:
                    lst = lists[e]
                    i = ptr[e]
                    cnt = 0
                    while i < len(lst) and cnt < (1 if e in FIXED else W):
                        o = lst[i]
                        i += 1
                        if id(o) in sched_flag:
                            continue
                        cnt += 1
                        ok = True
                        rdy = free[e]
                        for d in o.deps:
                            if d.seg != sg:
                                continue
                            if id(d) not in sched_flag:
                                ok = False
                                break
                            if d.fin > rdy:
                                rdy = d.fin
                        if not ok:
                            continue
                        key = (rdy, o.gidx)
                        if best is None or key < best[0]:
                            best = (key, e, o)
                        if rdy <= free[e]:
                            break
                assert best is not None, "scheduler stuck"
                (rdy, _), e, o = best
                if o.dma:
                    issue_end = rdy + 0.1
                    st_bw = max(issue_end, dma_free)
                    dma_free = st_bw + o.nbytes / BW
                    o.fin = dma_free + 2.0
                    free[e] = issue_end
                else:
                    o.fin = rdy + o.cost
                    free[e] = o.fin
                sched_flag[id(o)] = True
                final[e].append(o)
                remaining -= 1
                lst = lists[e]
                while ptr[e] < len(lst) and id(lst[ptr[e]]) in sched_flag:
                    ptr[e] += 1
            for e in self.ENG:
                free[e] = max([free[e]] + [o.fin for o in final[e][-1:]])
            prev_last = [final[e][-1] for e in self.ENG if final[e]]
            if os.environ.get("MK_STATS"):
                print("SEG", sg, "ends at us", max(free.values()), {e: round(sum(o.cost for o in lists[e]), 1) for e in self.ENG})
            if sg + 1 < nseg:
                for e in self.ENG:
                    for o in self.ops[e]:
                        if o.seg == sg + 1:
                            o.deps.update(prev_last)
        for e in self.ENG:
            self.ops[e] = final[e]
            for i, o in enumerate(final[e]):
                o.idx = i
        if os.environ.get("MK_STATS"):
            print("SCHED est makespan us", max(free.values()))

    def emit(self, nc, stack):
        if os.environ.get("MK_NOSCHED") is None:
            self.schedule()
        else:
            bars = {}
            for sg in range(self.seg):
                bars[sg + 1] = [[o for o in self.ops[e] if o.seg <= sg][-1] for e in self.ENG if [o for o in self.ops[e] if o.seg <= sg]]
            for e in self.ENG:
                for o in self.ops[e]:
                    if o.seg in bars:
                        o.deps.update(bars[o.seg])
        need = {}
        for e in self.ENG:
            for op in self.ops[e]:
                nd = []
                for d in op.deps:
                    if d.dma or d.eng != op.eng:
                        nd.append(d)
                    else:
                        if STRICT:
                            nd.append(d)
                            continue
                        if op.eng == "pe" and not op.dma:
                            continue
                        if op.dma or (d in op.raw and op.idx - d.idx <= 2):
                            nd.append(d)
                for d in nd:
                    d.inc = True
                need[op] = nd
        sems = {e: stack.enter_context(nc.semaphore("s_" + e)) for e in self.ENG}
        dsems = {e: [stack.enter_context(nc.semaphore("d_%s%d" % (e, i))) for i in range(self.NDMA)]
                 for e in ("sp", "pool")}
        for e in self.ENG:
            c = 0
            nd = 0
            for op in self.ops[e]:
                if op.dma:
                    op.dsem = (e, nd % self.NDMA)
                    op.dval = 16 * (nd // self.NDMA + 1)
                    nd += 1
                elif op.inc:
                    c += 1
                    op.cnt = c
        if os.environ.get("MK_STATS"):
            for e in self.ENG:
                print("ENG", e, "ops", len(self.ops[e]), "incs", sum(1 for o in self.ops[e] if o.inc),
                      "waits", sum(len(need[o]) for o in self.ops[e]))
        block = stack.enter_context(nc.Block())

        def run(e):
            def body(eng):
                waited = {}
                dmas = []
                for op in self.ops[e]:
                    pend = []
                    for d in sorted(need[op], key=lambda o: o.gidx):
                        if d.dma:
                            key = d.dsem; val = d.dval; sem = dsems[d.dsem[0]][d.dsem[1]]
                        else:
                            key = d.eng; val = d.cnt; sem = sems[d.eng]
                        if waited.get(key, 0) >= val:
                            continue
                        waited[key] = val
                        pend.append((sem, val))
                    last = None
                    if pend and not op.dma:
                        last = pend.pop()
                    for (sem, val) in pend:
                        eng.wait_ge(sem, val)
                    if op.dma:
                        k = len(dmas)
                        if k >= self.NDMA:
                            prev = dmas[k - self.NDMA]
                            if waited.get(prev.dsem, 0) < prev.dval:
                                waited[prev.dsem] = prev.dval
                                eng.wait_ge(dsems[prev.dsem[0]][prev.dsem[1]], prev.dval)
                        dmas.append(op)
                        op.fn(eng).then_inc(dsems[op.dsem[0]][op.dsem[1]], 16)
                    else:
                        ins = op.fn(eng)
                        if last is not None:
                            ins._wait_ge(last[0], last[1])
                        if op.inc:
                            ins.then_inc(sems[e], 1)
                for op in dmas[-self.NDMA:]:
                    if waited.get(op.dsem, 0) < op.dval:
                        waited[op.dsem] = op.dval
                        eng.wait_ge(dsems[op.dsem[0]][op.dsem[1]], op.dval)
            return body

        block.tensor(run("pe"))
        block.scalar(run("act"))
        block.vector(run("dve"))
        block.gpsimd(run("pool"))
        block.sync(run("sp"))


class Arena:
    def __init__(self, t, nwords):
        self.t = t; self.n = nwords; self.off = 0

    def mark(self):
        return self.off

    def reset(self, m):
        self.off = m

    def alloc(self, free_shape, dt):
        n_el = 1
        for s in free_shape:
            n_el *= s
        esz = 4 if dt == F32 else 2
        nw = (n_el * esz + 3) // 4
        nw = (nw + 7) // 8 * 8
        assert self.off + nw <= self.n, ("arena overflow", self.off, nw, self.n)
        ap = self.t[:, self.off:self.off + nw]
        self.off += nw
        if dt != F32:
            ap = ap.bitcast(dt)
        ap = ap[:, 0:n_el]
        if len(free_shape) == 2:
            ap = ap.rearrange("p (a b) -> p a b", a=free_shape[0])
        elif len(free_shape) == 3:
            ap = ap.rearrange("p (a b c) -> p a b c", a=free_shape[0], b=free_shape[1])
        elif len(free_shape) == 4:
            ap = ap.rearrange("p (a b c d) -> p a b c d", a=free_shape[0], b=free_shape[1], c=free_shape[2])
        return ap


def build_program():
    nc = bass.Bass("TRN2", target_bir_lowering=False)
    dr = lambda name, shape, dt=F32, kind="ExternalInput": nc.dram_tensor(name, shape, dt, kind=kind).ap()
    x_d = dr("x", [S_LEN, D]); ctx_d = dr("ctx", [L_CTX, D]); cc_d = dr("cc", [128, 16])
    wada_d = dr("w_ada", [D + 1, 6 * D])[0:D, :]; bada_d = dr("b_ada", [1, 6 * D])
    win_d = dr("w_in", [D + 1, 2320])[0:D, :]; wout_d = dr("w_out", [D + 1, D])[0:D, :]
    wfi_d = dr("w_ffn_in", [D + 1, 2 * FH])[0:D, :]; wfo_d = dr("w_ffn_out", [FH + 1, D])[0:FH, :]
    vfm_d = dr("vec_fm", [128, 80])
    vrow_d = dr("vec_row", [1, 2576])
    cf_d = dr("cst_f32", [128, 2, 128])
    cb_d = dr("cst_bf16", [128, 21, 128], BF16)
    rope_d = dr("rope", [128, NX * 128 + 1])[:, 0:NX * 128].rearrange("p (j c d) -> p j c d", j=NX, c=2)
    out_d = dr("out", [S_LEN, D], F32, "ExternalOutput")
    x1_d = dr("x1s", [S_LEN, D], F32, "ExternalOutput")

    S = Sched()
    st = ExitStack()
    with st:
        arena_t = st.enter_context(nc.sbuf_tensor("arena", [128, 53000], F32))
        AR = Arena(arena_t, 53000)
        psum_t = st.enter_context(nc.psum_tensor("psum", [128, 8, 512], F32))
        PB = [psum_t[:, i, :] for i in range(8)]
        TB = [T("bank%d" % i) for i in range(8)]
        PB0h = PB[0].bitcast(BF16)

        def fsz(ap):
            n = 1
            for d_ in ap.shape[1:]:
                n *= d_
            return n

        def c_act(ap):
            return (224.0 + fsz(ap)) / 1400.0

        def c_dve(ap):
            return (64.0 + fsz(ap)) / 960.0

        def dma(out, in_, R, W, q="sp"):
            nb = out.shape[0] * fsz(out) * (4 if out.dtype == F32 else 2)
            S.add(q, lambda e: e.dma_start(out=out, in_=in_), reads=R, writes=W, dma=True, nbytes=nb)

        def mm(out, lhsT, rhs, start, stop, R, W):
            S.add("pe", lambda e: e.matmul(out, lhsT=lhsT, rhs=rhs, start=start, stop=stop), reads=R, writes=W,
                  cost=max(fsz(rhs), 64) / 2000.0 + 0.04)

        def tr(out, in_, ident, R, W):
            S.add("pe", lambda e: e.transpose(out=out, in_=in_, identity=ident), reads=R, writes=W, cost=0.1)

        def act(out, in_, func, R, W, scale=None, bias=None, accum=None):
            kw = {}
            if scale is not None:
                kw["scale"] = scale
            if bias is not None:
                kw["bias"] = bias
            if accum is not None:
                kw["accum_out"] = accum
            S.add("act", lambda e: e.activation(out=out, in_=in_, func=func, **kw), reads=R, writes=W, cost=c_act(out))

        def tt(eng, out, in0, in1, op, R, W):
            S.add(eng, lambda e: e.tensor_tensor(out=out, in0=in0, in1=in1, op=op), reads=R, writes=W, cost=c_dve(out))

        def ts(eng, out, in0, s1, s2, op0, op1, R, W):
            if s2 is None:
                S.add(eng, lambda e: e.tensor_scalar(out=out, in0=in0, scalar1=s1, scalar2=None, op0=op0), reads=R, writes=W, cost=c_dve(out))
            else:
                S.add(eng, lambda e: e.tensor_scalar(out=out, in0=in0, scalar1=s1, scalar2=s2, op0=op0, op1=op1), reads=R, writes=W, cost=c_dve(out))

        def stt(eng, out, in0, scalar, in1, op0, op1, R, W):
            S.add(eng, lambda e: e.scalar_tensor_tensor(out=out, in0=in0, scalar=scalar, in1=in1, op0=op0, op1=op1),
                  reads=R, writes=W, cost=c_dve(out))

        def cp(eng, out, in_, R, W):
            if eng == "act":
                act(out, in_, AF.Copy, R, W)
            else:
                S.add(eng, lambda e: e.tensor_copy(out=out, in_=in_), reads=R, writes=W, cost=c_dve(out))

        def recip(out, in_, R, W):
            S.add("dve", lambda e: e.reciprocal(out=out, in_=in_), reads=R, writes=W, cost=c_dve(out))

        def memset(eng, ap, val, W):
            S.add(eng, lambda e: e.memset(ap, val), writes=W, cost=c_dve(ap))

        def bc(ap, shape):
            return ap.to_broadcast(shape)

        def rstd_ops(out, ssq, tmp, inv_n, R, W, Tt):
            ts("dve", tmp, ssq, inv_n, EPS, ALU.mult, ALU.add, R, [Tt])
            act(tmp, tmp, AF.Ln, [Tt], [Tt])
            act(out, tmp, AF.Exp, [Tt], W, scale=-0.5)

        def dump(name, ap, R):
            if not DEBUG or name not in os.environ.get('MK_DUMP', '').split(','):
                return
            dd = nc.dram_tensor("dbg_" + name, list(ap.shape), ap.dtype, kind="ExternalOutput").ap()
            dma(dd, ap, R, [T()])

        cf = AR.alloc([2, 128], F32); Tcf = T()
        cb = AR.alloc([21, 128], BF16); Tcb = T()
        tri_f = cf[:, 0, :]; tri_b = cf[:, 1, :]
        identb = cb[:, 0, :]; onesb = cb[:, 1, :]
        mask_prev = cb[:, 2:6, :]; mask_next = cb[:, 6:10, :]
        nstr_f = cb[:, 10, :]; nstr_b = cb[:, 11, :]; negones = cb[:, 12, :]
        madd_f4 = cb[:, 13:17, :]; madd_b4 = cb[:, 17:21, :]
        vfm = AR.alloc([80], F32); Tvfm = T()
        badaT = vfm[:, 32:80]
        gpre_m = vfm[:, 0:8]; gpre_f = vfm[:, 8:16]; wconv = vfm[:, 16:28].rearrange("p (c k) -> p c k", c=4)
        esink = AR.alloc([4], F32); Tes = T()
        cc = AR.alloc([8, 2], F32); Tcc = T()
        scc = AR.alloc([8, 2], F32); Tscc = T()
        sccb = AR.alloc([8, 2], BF16); Tsccb = T()
        sccx = AR.alloc([8, 128], BF16); Tsccx = T()
        modT = AR.alloc([48, 2], F32); TmodT = T()
        Gm = AR.alloc([8, 2], F32); TGm = T()
        Gf = AR.alloc([8, 2], F32); TGf = T()
        ggm = AR.alloc([1024], F32); Tggm = T()
        ggf = AR.alloc([1024], F32); Tggf = T()
        gain_bc = AR.alloc([512], F32); Tgain = T()
        bg_bc = AR.alloc([16], F32); Tbg = T()
        stat = AR.alloc([NT, 4], F32)
        Tstat = [T() for _ in range(NT)]

        dma(cf, cf_d, [], [Tcf]); dma(cb, cb_d, [], [Tcb]); dma(vfm, vfm_d, [], [Tvfm])
        dma(cc, cc_d.rearrange("p (k t) -> p k t", k=8), [], [Tcc])
        act(scc, cc, AF.Silu, [Tcc], [Tscc])
        cp("dve", sccb, scc, [Tscc], [Tsccb])
        cp("dve", sccx, bc(scc[:, :, 0:1], [128, 8, 128]), [Tscc], [Tsccx])
        act(esink, vfm[:, 28:32], AF.Exp, [Tvfm], [Tes])

        m_persist = AR.mark()
        qT = AR.alloc([4, S_LEN], BF16); TqT = [T() for _ in range(NX)]
        kT = AR.alloc([S_LEN], BF16); TkT = [T() for _ in range(NX)]
        kcT = AR.alloc([L_CTX], BF16); TkcT = [T() for _ in range(2)]
        Vst = AR.alloc([NT, 128], BF16); TV = [T() for _ in range(NT)]
        mqk = AR.alloc([4, NT * 128], BF16); Tmqk = [T() for _ in range(NT)]
        mKt = AR.alloc([NT, 256], BF16); TmKt = [T() for _ in range(NT)]
        mVa = AR.alloc([NT, 4, 129], BF16); TmVa = [T() for _ in range(NT)]
        gsig = AR.alloc([NX, 512], BF16); Tgsig = [T() for _ in range(NX)]
        gts = AR.alloc([NT, 16], F32); Tgts = [T() for _ in range(NT)]
        nlf = AR.alloc([NT, 2, 4], F32); Tnlf = T()
        nlfh = AR.alloc([NT, 2, 4], BF16); nlfl = AR.alloc([NT, 2, 4], BF16); Tnlfs = T()
        gam = AR.alloc([NT, 2, 4], F32); Tgam = T()
        rope = AR.alloc([NX, 2, 64], F32); Trope = T()
        dma(rope, rope_d, [], [Trope])
        TmVa_all = T()
        S.add("dve", lambda e: e.memset(mVa, 1.0), writes=TmVa + [TmVa_all])

        m_region = AR.mark()

        stg = [AR.alloc([1160], F32) for _ in range(2)]; Tstg = [T(), T()]
        win = AR.alloc([8, 2320], BF16); Twin = [T() for _ in range(8)]
        rowb = AR.alloc([1024], F32); Trowb = T()
        wab = [AR.alloc([1024], BF16) for _ in range(2)]; Twab = [T(), T()]
        xin = [AR.alloc([1024], F32) for _ in range(2)]; Txin = [T(), T()]
        junk = AR.alloc([1024], BF16); Tjunk = T()
        xn = [AR.alloc([1024], BF16) for _ in range(2)]; Txn = [T(), T()]
        ynT = [AR.alloc([8, 128], BF16) for _ in range(2)]; TynT = [T(), T()]
        ropeA = AR.alloc([640], F32); TrA = T()
        ropeB = AR.alloc([640], F32); TrB = T()
        qtok = AR.alloc([640], BF16); Tqtok = T()
        raw = AR.alloc([3, 4, 130], F32); Traw = [T(), T(), T()]
        cacc = AR.alloc([4, 128], F32); Tcacc = T()
        ctmp = AR.alloc([4, 128], F32); Tctmp = T()
        gtmp = AR.alloc([NT, 2, 4], F32); Tgtmp = T()

        def ada_load(seg, k):
            b = k % 2
            dma(stg[b][:, 0:1024], wada_d[k * 128:(k + 1) * 128, seg * 1024:(seg + 1) * 1024], [], [Tstg[b]])
            cp("act", wab[b], stg[b][:, 0:1024], [Tstg[b]], [Twab[b]])
            return b

        def ada_segment(seg):
            wada_v = wada_d.rearrange("(k p) n -> p k n", p=128)
            for c8 in range(8):
                b = c8 % 2
                col0 = seg * 1024 + c8 * 128
                dma(stg[b][:, 0:1024].rearrange("p (k n) -> p k n", k=8), wada_v[:, :, col0:col0 + 128], [], [Tstg[b]])
                cp("act", wab[b], stg[b][:, 0:1024], [Tstg[b]], [Twab[b]])
                for k in range(8):
                    mm(PB[5][:, c8 * 2:(c8 + 1) * 2], wab[b][:, k * 128:(k + 1) * 128], sccb[:, k, :], k == 0, k == 7,
                       [Tsccb, Twab[b]], [TB[5]])
            tt("dve", modT[:, seg * 8:(seg + 1) * 8, :], PB[5][:, 0:16].rearrange("p (a b) -> p a b", a=8),
               bc(badaT[:, seg * 8:(seg + 1) * 8].unsqueeze(2), [128, 8, 2]), ALU.add, [TB[5], Tvfm], [TmodT])

        def ada_row(seg, dst, Tdst_):
            dma(rowb, bada_d[0:1, seg * 1024:(seg + 1) * 1024].to_broadcast([128, 1024]), [], [Trowb])
            for k in range(8):
                b = ada_load(seg, k)
                for hf in range(2):
                    mm(PB[6 + hf], sccx[:, k, :], wab[b][:, hf * 512:(hf + 1) * 512], k == 0, k == 7,
                       [Tsccx, Twab[b]], [TB[6 + hf]])
            for hf in range(2):
                cs = slice(hf * 512, (hf + 1) * 512)
                tt("dve", rowb[:, cs], PB[6 + hf], rowb[:, cs], ALU.add, [TB[6 + hf], Trowb], [Trowb])
                tt("dve", dst[:, cs], dst[:, cs], rowb[:, cs], ALU.mult, [Trowb, Tdst_], [Tdst_])

        ada_segment(0)
        ada_segment(1)
        stt("dve", Gm, modT[:, 8:16, :], 1.0, bc(gpre_m.unsqueeze(2), [128, 8, 2]), ALU.add, ALU.mult, [TmodT, Tvfm], [TGm])
        dma(gain_bc, vrow_d[0:1, 2048:2560].to_broadcast([128, 512]), [], [Tgain])
        dma(bg_bc, vrow_d[0:1, 2560:2576].to_broadcast([128, 16]), [], [Tbg])
        dma(ggm, vrow_d[0:1, 0:1024].to_broadcast([128, 1024]), [], [Tggm])
        dma(ggf, vrow_d[0:1, 1024:2048].to_broadcast([128, 1024]), [], [Tggf])

        if STOP <= 0:
            S.emit(nc, st)
            return nc
        for k in range(8):
            for hh in range(2):
                dma(stg[hh], win_d[k * 128:(k + 1) * 128, hh * 1160:(hh + 1) * 1160], [], [Tstg[hh]])
                cp("act", win[:, k, hh * 1160:(hh + 1) * 1160], stg[hh], [Tstg[hh]], [Twin[k]])

        def conv_tile(tau):
            sl = tau % 3
            rw = raw[:, sl]
            w0 = bc(wconv[:, :, 0:1], [128, 4, 128]); w1 = bc(wconv[:, :, 1:2], [128, 4, 128]); w2 = bc(wconv[:, :, 2:3], [128, 4, 128])
            tt("dve", cacc, rw[:, :, 1:129], w1, ALU.mult, [Traw[sl], Tvfm], [Tcacc])
            tt("dve", ctmp, rw[:, :, 0:128], w0, ALU.mult, [Traw[sl], Tvfm], [Tctmp])
            tt("dve", cacc, cacc, ctmp, ALU.add, [Tcacc, Tctmp], [Tcacc])
            tt("dve", ctmp, rw[:, :, 2:130], w2, ALU.mult, [Traw[sl], Tvfm], [Tctmp])
            tt("dve", cacc, cacc, ctmp, ALU.add, [Tcacc, Tctmp], [Tcacc])
            act(mqk[:, :, tau * 128:(tau + 1) * 128], cacc, AF.Silu, [Tcacc], [Tmqk[tau]])
            for pr in range(2):
                tr(PB0h[:, pr * 128:(pr + 1) * 128], mqk[:, 2 + pr, tau * 128:(tau + 1) * 128], identb, [Tmqk[tau], Tcb], [TB[0]])
            cp("act", mKt[:, tau, :], PB0h[:, 0:256], [TB[0]], [TmKt[tau]])

        def phaseA_front(tau):
            is_ctx = tau < 2
            b = tau % 2
            src = ctx_d[tau * 128:(tau + 1) * 128, :] if is_ctx else x_d[(tau - 2) * 128:(tau - 1) * 128, :]
            dma(xin[b], src, [], [Txin[b]])
            sq = stat[:, tau, :]
            act(junk, xin[b], AF.Square, [Txin[b]], [Tjunk, Tstat[tau]], accum=sq[:, 0:1])
            rstd_ops(sq[:, 1:2], sq[:, 0:1], sq[:, 2:3], 1.0 / D, [Tstat[tau]], [Tstat[tau]], Tstat[tau])
            ts("dve", xn[b], xin[b], sq[:, 1:2], None, ALU.mult, None, [Txin[b], Tstat[tau]], [Txn[b]])
            for k in range(8):
                tr(PB0h[:, k * 128:(k + 1) * 128], xn[b][:, k * 128:(k + 1) * 128], identb, [Txn[b], Tcb], [TB[0]])
            mi = 1 if is_ctx else 0
            for k in range(8):
                act(ynT[b][:, k, :], PB0h[:, k * 128:(k + 1) * 128], AF.Identity, [TB[0], TGm, TmodT], [TynT[b]],
                    scale=Gm[:, k, mi:mi + 1], bias=modT[:, k, mi:mi + 1])

        def phaseA_rest(tau):
            is_ctx = tau < 2
            b = tau % 2
            if not is_ctx:
                for k in range(8):
                    mm(PB[1], ynT[b][:, k, :], win[:, k, 0:512], k == 0, k == 7, [TynT[b], Twin[k]], [TB[1]])
            for k in range(8):
                mm(PB[2][:, 0:272], ynT[b][:, k, :], win[:, k, 512:784], k == 0, k == 7, [TynT[b], Twin[k]], [TB[2]])
            for k in range(8):
                mm(PB[3], ynT[b][:, k, :], win[:, k, 784:1296], k == 0, k == 7, [TynT[b], Twin[k]], [TB[3]])
            if not is_ctx:
                for k in range(8):
                    mm(PB[4], ynT[b][:, k, :], win[:, k, 1296:1808], k == 0, k == 7, [TynT[b], Twin[k]], [TB[4]])
            for c in range(4):
                for k in range(8):
                    mm(PB[5][:, c * 128:(c + 1) * 128], win[:, k, 1808 + c * 128:1808 + (c + 1) * 128], ynT[b][:, k, :],
                       k == 0, k == 7, [TynT[b], Twin[k]], [TB[5]])
            cp("act", Vst[:, tau, :], PB[2][:, 128:256], [TB[2]], [TV[tau]])
            tt("dve", gts[:, tau, :], PB[2][:, 256:272], bg_bc, ALU.add, [TB[2], Tbg], [Tgts[tau]])
            cp("act", mVa[:, tau, :, 0:128], PB[3].rearrange("p (h v) -> p h v", h=4), [TB[3], TmVa_all], [TmVa[tau]])
            sl = tau % 3
            cp("dve", raw[:, sl, :, 1:129], PB[5].rearrange("p (c t) -> p c t", c=4), [TB[5]], [Traw[sl]])
            first = tau in (0, 2)
            last = tau in (1, NT - 1)
            if first:
                memset("dve", raw[:, sl, :, 0:1], 0.0, [Traw[sl]])
            else:
                pl = (tau - 1) % 3
                cp("dve", raw[:, sl, :, 0:1], raw[:, pl, :, 128:129], [Traw[pl], Traw[sl]], [Traw[sl]])
                cp("dve", raw[:, pl, :, 129:130], raw[:, sl, :, 1:2], [Traw[sl], Traw[pl]], [Traw[pl]])
                conv_tile(tau - 1)
            if last:
                memset("dve", raw[:, sl, :, 129:130], 0.0, [Traw[sl]])
                conv_tile(tau)
            if is_ctx:
                cp("dve", qtok[:, 512:640], PB[2][:, 0:128], [TB[2]], [Tqtok])
                tr(PB0h[:, 0:128], qtok[:, 512:640], identb, [Tqtok, Tcb], [TB[0]])
                cp("act", kcT[:, tau * 128:(tau + 1) * 128], PB0h[:, 0:128], [TB[0]], [TkcT[tau]])
            else:
                j = tau - 2
                act(junk[:, 0:512], PB[4], AF.Sigmoid, [TB[4]], [Tjunk])
                tt("dve", gsig[:, j, :], junk[:, 0:512], gain_bc, ALU.mult, [Tjunk, Tgain], [Tgsig[j]])
                cosb8 = bc(rope[:, j, 0:1, :], [128, 8, 64]); sinb8 = bc(rope[:, j, 1:2, :], [128, 8, 64])
                cosb2 = bc(rope[:, j, 0:1, :], [128, 2, 64]); sinb2 = bc(rope[:, j, 1:2, :], [128, 2, 64])
                A3 = ropeA.rearrange("p (h d) -> p h d", h=10); B3 = ropeB.rearrange("p (h d) -> p h d", h=10)
                tt("dve", A3[:, 0:8, :], PB[1].rearrange("p (h d) -> p h d", h=8), cosb8, ALU.mult, [TB[1], Trope], [TrA])
                tt("dve", B3[:, 0:8, :], PB[1].rearrange("p (h d) -> p h d", h=8), sinb8, ALU.mult, [TB[1], Trope], [TrB])
                tt("dve", A3[:, 8:10, :], PB[2][:, 0:128].rearrange("p (h d) -> p h d", h=2), cosb2, ALU.mult, [TB[2], Trope], [TrA])
                tt("dve", B3[:, 8:10, :], PB[2][:, 0:128].rearrange("p (h d) -> p h d", h=2), sinb2, ALU.mult, [TB[2], Trope], [TrB])
                A4 = ropeA.rearrange("p (g h d) -> p g h d", g=20, h=2); B4 = ropeB.rearrange("p (g h d) -> p g h d", g=20, h=2)
                Q4 = qtok.rearrange("p (g h d) -> p g h d", g=20, h=2)
                tt("dve", Q4[:, :, 0, :], A4[:, :, 0, :], B4[:, :, 1, :], ALU.subtract, [TrA, TrB], [Tqtok])
                tt("dve", Q4[:, :, 1, :], A4[:, :, 1, :], B4[:, :, 0, :], ALU.add, [TrA, TrB], [Tqtok])
                for i in range(5):
                    tr(PB0h[:, i * 128:(i + 1) * 128], qtok[:, i * 128:(i + 1) * 128], identb, [Tqtok, Tcb], [TB[0]])
                cp("act", qT[:, :, j * 128:(j + 1) * 128], PB0h[:, 0:512].rearrange("p (i t) -> p i t", i=4), [TB[0]], [TqT[j]])
                cp("dve", kT[:, j * 128:(j + 1) * 128], PB0h[:, 512:640], [TB[0]], [TkT[j]])
            if tau == 3:
                ada_row(2, ggm, Tggm)
            if tau == 6:
                ada_segment(3)
            if tau == 8:
                ada_segment(4)
                stt("dve", Gf, modT[:, 32:40, :], 1.0, bc(gpre_f.unsqueeze(2), [128, 8, 2]), ALU.add, ALU.mult,
                    [TmodT, Tvfm], [TGf])
            if tau == 10:
                ada_row(5, ggf, Tggf)


        phaseA_front(0)
        for tau in range(NT):
            if tau + 1 < NT:
                phaseA_front(tau + 1)
            phaseA_rest(tau)

        if STOP <= 1:
            S.emit(nc, st)
            return nc
        g5 = gts.rearrange("p n (d i h) -> p n d i h", d=2, i=2)
        Tg_all = Tgts
        act(gtmp, g5[:, :, :, 1, :], AF.Exp, Tg_all, [Tgtmp], scale=-1.0)
        ts("dve", gtmp, gtmp, 1.0, None, ALU.add, None, [Tgtmp], [Tgtmp])
        act(nlf, gtmp, AF.Ln, [Tgtmp], [Tnlf])
        cp("dve", nlfh, nlf, [Tnlf], [Tnlfs])
        tt("dve", nlfl, nlf, nlfh, ALU.subtract, [Tnlf, Tnlfs], [Tnlfs])
        mm(PB[7][:, 0:NT * 8], negones, nlfh.rearrange("p n d h -> p (n d h)"), True, False, [Tnlfs, Tcb], [TB[7]])
        mm(PB[7][:, 0:NT * 8], negones, nlfl.rearrange("p n d h -> p (n d h)"), False, True, [Tnlfs, Tcb], [TB[7]])
        act(gam.rearrange("p n d h -> p (n d h)"), PB[7][:, 0:NT * 8], AF.Exp, [TB[7]], [Tgam])

        dump("modT", modT, [TmodT]); dump("Gm", Gm, [TGm]); dump("ggm", ggm, [Tggm]); dump("gts", gts, Tgts)
        dump("qT", qT[:, :, 0:256], TqT); dump("kT", kT[:, 0:256], TkT); dump("kcT", kcT, TkcT); dump("Vst", Vst[:, 0:4], TV)
        dump("mqk", mqk[:, :, 0:512], Tmqk); dump("mKt", mKt[:, 0:4], TmKt); dump("mVa", mVa[:, 0:4], TmVa); dump("gsig", gsig[:, 0:2], Tgsig)
        dump("nlf", nlf, [Tnlf]); dump("gam", gam, [Tgam])
        if os.environ.get('MK_STATS'):
            print('ARENA end phase A', AR.off, 'persist', m_persist, 'region', m_region)
        S.barrier()
        AR.reset(m_region)

        Sst = AR.alloc([2, NX, 2, 129], BF16); TSst = [[T() for _ in range(NX)] for _ in range(2)]
        wout = AR.alloc([8, 1024], BF16); Twout = [T() for _ in range(8)]
        stg2 = [AR.alloc([1024], F32) for _ in range(2)]; Tstg2 = [T(), T()]
        Cst = AR.alloc([2, 129], F32); TCst = T()
        wk = AR.alloc([4], F32); Twk = T()
        Vw = AR.alloc([4, 129], BF16); TVw = T()
        for k in range(8):
            b = k % 2
            dma(stg2[b], wout_d[k * 128:(k + 1) * 128, :], [], [Tstg2[b]])
            cp("act", wout[:, k, :], stg2[b], [Tstg2[b]], [Twout[k]])

        def scan_step(tau, d, store_j):
            nstr = nstr_f if d == 0 else nstr_b
            mm(PB[4][:, 0:4], nstr, nlfh[:, tau, d, :], True, False, [Tnlfs, Tcb], [TB[4]])
            mm(PB[4][:, 0:4], nstr, nlfl[:, tau, d, :], False, True, [Tnlfs, Tcb], [TB[4]])
            tt("dve", wk, PB[4][:, 0:4], g5[:, tau, d, 0, :], ALU.add, [TB[4], Tgts[tau]], [Twk])
            act(wk, wk, AF.Exp, [Twk], [Twk])
            tt("dve", Vw, mVa[:, tau], bc(wk.unsqueeze(2), [128, 4, 129]), ALU.mult, [TmVa[tau], Twk], [TVw])
            for pr in range(2):
                for eo in range(2):
                    mm(PB[pr][:, eo * 129:(eo + 1) * 129], mKt[:, tau, pr * 128:(pr + 1) * 128], Vw[:, 2 * pr + eo, :],
                       True, True, [TmKt[tau], TVw], [TB[pr]])
            for hp in range(2):
                ps_ = slice(hp * 64, (hp + 1) * 64)
                for pr in range(2):
                    h = 2 * pr + hp
                    stt("dve", Cst[ps_, pr, :], Cst[ps_, pr, :], gam[ps_, tau, d, h:h + 1], PB[pr][ps_, hp * 129:(hp + 1) * 129],
                        ALU.mult, ALU.add, [TCst, Tgam, TB[pr]], [TCst])
            if store_j is not None:
                cp("act", Sst[:, d, store_j], Cst, [TCst], [TSst[d][store_j]])

        memset("dve", Cst, 0.0, [TCst])
        for tau in range(0, NT - 1):
            scan_step(tau, 0, tau - 1 if tau >= 1 else None)
        memset("dve", Cst, 0.0, [TCst])
        for tau in [1, 0] + list(range(NT - 1, 2, -1)):
            if tau == 1:
                sj = None
            elif tau == 0:
                sj = NX - 1
            else:
                sj = tau - 3
            scan_step(tau, 1, sj)

        dump("Sst0", Sst[:, 0], TSst[0]); dump("Sst1", Sst[:, 1], TSst[1])
        if STOP <= 2:
            S.emit(nc, st)
            return nc
        rhsG = AR.alloc([4, 128], F32); TrhsG = T()
        rgh = [[AR.alloc([4, 128], BF16) for _ in range(2)] for _ in range(2)]
        rgl = [[AR.alloc([4, 128], BF16) for _ in range(2)] for _ in range(2)]
        Trgs = [[T(), T()], [T(), T()]]
        Dm = [AR.alloc([4, 128], F32) for _ in range(2)]; TDm = [T(), T()]
        abc = [AR.alloc([2, 128], F32) for _ in range(2)]; Tabc = [T(), T()]
        STm = [AR.alloc([4, 128], BF16) for _ in range(2)]; TSTm = [T(), T()]
        qtl = [AR.alloc([2, 2, 128], BF16) for _ in range(2)]; Tqtl = [T(), T()]
        qz = [AR.alloc([2, 2, 128], BF16) for _ in range(2)]; Tqz = [T(), T()]
        qaz = [AR.alloc([2, 4, 128], BF16) for _ in range(2)]; Tqaz = [T(), T()]
        memset("dve", qtl[0], 0.0, [Tqtl[0]]); memset("dve", qtl[1], 0.0, [Tqtl[1]])
        for p_ in range(2):
            memset("dve", qz[p_], 0.0, [Tqz[p_]]); memset("dve", qaz[p_], 0.0, [Tqaz[p_]])
        rden = AR.alloc([2, 4], F32); Trden = T()
        hsum = AR.alloc([4, 128], F32); Thsum = T()
        htmp = AR.alloc([4, 128], F32); Thtmp = T()
        hst = AR.alloc([16], F32); Thst = T()
        mltok = AR.alloc([512], BF16); Tmltok = T()
        mlT = AR.alloc([4, 128], BF16); TmlT = T()
        attT = AR.alloc([4, 128], BF16); TattT = T()
        PT = [AR.alloc([512], BF16) for _ in range(2)]; TPT = [T(), T()]
        zt = AR.alloc([512], F32); Tzt = T()
        xc = stg2; Txc = Tstg2
        x1t = [AR.alloc([1024], F32) for _ in range(2)]; Tx1t = [T(), T()]
        cst = AR.alloc([NX, 4], F32); Tcst = [T() for _ in range(NX)]
        Tx1d = [T() for _ in range(NX)]
        ptc = [0]
        if os.environ.get('MK_STATS'):
            print('ARENA end phase C alloc', AR.off)

        def c_prep(j):
            tau = j + 2; par = j % 2
            cols = slice(tau * 128, (tau + 1) * 128)
            qcols = slice(j * 128, (j + 1) * 128)
            for d in range(2):
                tri = tri_f if d == 0 else tri_b
                for h in range(4):
                    ts("dve", rhsG[:, h, :], tri, nlf[:, tau, d, h:h + 1], None, ALU.mult, None, [Tnlf, Tcf], [TrhsG])
                cp("dve", rgh[par][d], rhsG, [TrhsG], [Trgs[par][d]])
                tt("dve", rgl[par][d], rhsG, rgh[par][d], ALU.subtract, [TrhsG, Trgs[par][d]], [Trgs[par][d]])
            for hp in range(2):
                ps_ = slice(hp * 64, (hp + 1) * 64)
                cp("dve", qz[par][ps_, :, hp, :], mqk[ps_, 0:2, cols], [Tmqk[tau]], [Tqz[par]])
            for g in range(2):
                gs = slice(g * 64, (g + 1) * 64)
                cp("dve", qaz[par][gs, g], qT[gs, :, qcols], [TqT[j]], [Tqaz[par]])

        def c_gates(j):
            tau = j + 2; par = j % 2
            cols = slice(tau * 128, (tau + 1) * 128)
            for d in range(2):
                nstr = nstr_f if d == 0 else nstr_b
                madd4 = madd_f4 if d == 0 else madd_b4
                rh = rgh[par][d].rearrange("p h t -> p (h t)"); rl = rgl[par][d].rearrange("p h t -> p (h t)")
                mm(PB[1], nstr, rh, True, False, [Trgs[par][d], Tcb], [TB[1]])
                mm(PB[1], nstr, rl, False, False, [Trgs[par][d], Tcb], [TB[1]])
                mm(PB[1], identb, madd4.rearrange("p h t -> p (h t)"), False, True, [Tcb], [TB[1]])
                mm(PB[2], negones, rh, True, False, [Trgs[par][d], Tcb], [TB[2]])
                mm(PB[2], negones, rl, False, True, [Trgs[par][d], Tcb], [TB[2]])
                for h in range(4):
                    act(Dm[d][:, h, :], PB[1][:, h * 128:(h + 1) * 128], AF.Exp, [TB[1], Tgts[tau]], [TDm[d]],
                        bias=g5[:, tau, d, 0, h:h + 1])
                p2 = PB[2].rearrange("p (pr hh t) -> p pr hh t", pr=2, hh=2)
                act(abc[d][0:64], p2[0:64, :, 0, :], AF.Exp, [TB[2]], [Tabc[d]])
                act(abc[d][64:128], p2[64:128, :, 1, :], AF.Exp, [TB[2]], [Tabc[d]])
            for h in range(4):
                hp = h % 2; pr = h // 2
                mm(PB[3][:, h * 128:(h + 1) * 128], mqk[:, 2 + pr, cols], qz[par][:, pr, hp, :], True, True,
                   [Tmqk[tau], Tqz[par]], [TB[3]])
            for d in range(2):
                stt("dve", STm[d].rearrange("p h t -> p (h t)"), PB[3], 0.125, Dm[d].rearrange("p h t -> p (h t)"),
                    ALU.mult, ALU.mult, [TB[3], TDm[d]], [TSTm[d]])
                for hp in range(2):
                    ps_ = slice(hp * 64, (hp + 1) * 64)
                    stt("dve", qtl[d][ps_, :, hp, :], mqk[ps_, 0:2, cols], 0.125, abc[d][ps_], ALU.mult, ALU.mult,
                        [Tmqk[tau], Tabc[d]], [Tqtl[d]])

        def c_attn(j):
            tau = j + 2; par = j % 2
            for g in range(2):
                gs = slice(g * 64, (g + 1) * 64)
                blocks = []
                if j > 0:
                    blocks.append(("prev", kT[:, (j - 1) * 128:j * 128], Vst[:, tau - 1, :], [TkT[j - 1], TV[tau - 1]]))
                blocks.append(("cur", kT[:, j * 128:(j + 1) * 128], Vst[:, tau, :], [TkT[j], TV[tau]]))
                if j < NX - 1:
                    blocks.append(("next", kT[:, (j + 1) * 128:(j + 2) * 128], Vst[:, tau + 1, :], [TkT[j + 1], TV[tau + 1]]))
                for c in range(2):
                    blocks.append(("ctx", kcT[:, c * 128:(c + 1) * 128], Vst[:, c, :], [TkcT[c], TV[c]]))
                nb = len(blocks)
                for bi, (kind, kap, vap, Rk) in enumerate(blocks):
                    pb = ptc[0] % 2; ptc[0] += 1
                    msk = kind in ("prev", "next")
                    mm(PB[6], kap, qaz[par][:, g].rearrange("p i t -> p (i t)"), True, not msk, Rk + [Tqaz[par]], [TB[6]])
                    if msk:
                        mk_ = (mask_prev if kind == "prev" else mask_next).rearrange("p i t -> p (i t)")
                        mm(PB[6], identb, mk_, False, True, [Tcb], [TB[6]])
                    act(PT[pb], PB[6], AF.Exp, [TB[6]], [TPT[pb]], scale=0.125)
                    mm(PB[7], vap, PT[pb], bi == 0, bi == nb - 1, Rk + [TPT[pb]], [TB[7]])
                    mm(PB[3], onesb, PT[pb], bi == 0, bi == nb - 1, [Tcb, TPT[pb]], [TB[3]])
                z3 = zt.rearrange("p (i t) -> p i t", i=4)
                tt("dve", z3[gs], PB[3][gs].rearrange("p (i t) -> p i t", i=4), bc(esink[gs].unsqueeze(2), [64, 4, 128]),
                   ALU.add, [TB[3], Tes], [Tzt])
                recip(zt[gs], zt[gs], [Tzt], [Tzt])
                tt("dve", attT[gs].rearrange("p i t -> p (i t)"), PB[7][gs], zt[gs], ALU.mult, [TB[7], Tzt], [TattT])

        def c_num(j):
            tau = j + 2
            for d in range(2):
                for h in range(4):
                    hp = h % 2; pr = h // 2
                    o = PB[4 + pr][:, hp * 129:(hp + 1) * 129]
                    mm(o, STm[d][:, h, :], mVa[:, tau, h, :], True, False, [TSTm[d], TmVa[tau]], [TB[4 + pr]])
                    mm(o, qtl[d][:, pr, hp, :], Sst[:, d, j, pr, :], False, True, [Tqtl[d], TSst[d][j]], [TB[4 + pr]])
                for pr in range(2):
                    pv = PB[4 + pr][:, 0:258].rearrange("p (hh c) -> p hh c", hh=2)
                    ts("dve", rden[:, pr, 0:2], pv[:, :, 128], -1.0, 1.0, ALU.mult, ALU.max, [TB[4 + pr]], [Trden])
                    tt("dve", rden[:, pr, 0:2], pv[:, :, 128], rden[:, pr, 0:2], ALU.max, [TB[4 + pr], Trden], [Trden])
                    recip(rden[:, pr, 2:4], rden[:, pr, 0:2], [Trden], [Trden])
                    dst = (hsum if d == 0 else htmp)[:, 2 * pr:2 * pr + 2, :]
                    tt("dve", dst, pv[:, :, 0:128], bc(rden[:, pr, 2:4].unsqueeze(2), [128, 2, 128]), ALU.mult,
                       [TB[4 + pr], Trden], [Thsum if d == 0 else Thtmp])
            tt("dve", hsum, hsum, htmp, ALU.add, [Thsum, Thtmp], [Thsum])
            for h in range(4):
                act(htmp[:, h, :], hsum[:, h, :], AF.Square, [Thsum], [Thtmp, Thst], accum=hst[:, h:h + 1])
            ts("dve", hst[:, 4:8], hst[:, 0:4], 1.0 / 128, EPS, ALU.mult, ALU.add, [Thst], [Thst])
            act(hst[:, 8:12], hst[:, 4:8], AF.Ln, [Thst], [Thst])
            act(hst[:, 12:16], hst[:, 8:12], AF.Exp, [Thst], [Thst], scale=-0.5)
            tt("dve", htmp, hsum, bc(hst[:, 12:16].unsqueeze(2), [128, 4, 128]), ALU.mult, [Thsum, Thst], [Thtmp])
            tt("dve", mltok, htmp.rearrange("p h v -> p (h v)"), gsig[:, j, :], ALU.mult, [Thtmp, Tgsig[j]], [Tmltok])
            for i in range(4):
                tr(PB0h[:, i * 128:(i + 1) * 128], mltok[:, i * 128:(i + 1) * 128], identb, [Tmltok, Tcb], [TB[0]])
            cp("act", mlT.rearrange("p i t -> p (i t)"), PB0h[:, 0:512], [TB[0]], [TmlT])

        def c_out(j):
            b = j % 2
            dma(xc[b], x_d[j * 128:(j + 1) * 128, :], [], [Txc[b]])
            for hf in range(2):
                for kk in range(8):
                    lhs = attT[:, kk, :] if kk < 4 else mlT[:, kk - 4, :]
                    mm(PB[4 + hf], lhs, wout[:, kk, hf * 512:(hf + 1) * 512], kk == 0, kk == 7,
                       [TattT, TmlT, Twout[kk]], [TB[4 + hf]])
            sq = cst[:, j, :]
            for hf in range(2):
                act(x1t[b][:, hf * 512:(hf + 1) * 512], PB[4 + hf], AF.Square, [TB[4 + hf]], [Tx1t[b], Tcst[j]],
                    accum=sq[:, hf:hf + 1])
            tt("dve", sq[:, 2:3], sq[:, 0:1], sq[:, 1:2], ALU.add, [Tcst[j]], [Tcst[j]])
            rstd_ops(sq[:, 3:4], sq[:, 2:3], sq[:, 0:1], 1.0 / D, [Tcst[j]], [Tcst[j]], Tcst[j])
            for hf in range(2):
                cs = slice(hf * 512, (hf + 1) * 512)
                stt("dve", x1t[b][:, cs], PB[4 + hf], sq[:, 3:4], ggm[:, cs], ALU.mult, ALU.mult,
                    [TB[4 + hf], Tcst[j], Tggm], [Tx1t[b]])
            tt("dve", x1t[b], x1t[b], xc[b], ALU.add, [Tx1t[b], Txc[b]], [Tx1t[b]])
            dma(x1_d[j * 128:(j + 1) * 128, :], x1t[b], [Tx1t[b]], [Tx1d[j]])

        try:
            if STOP <= 3.5:
                raise StopBuild()
            c_prep(0)
            for j in range(NX):
                c_gates(j)
                if j + 1 < NX:
                    c_prep(j + 1)
                c_attn(j)
                c_num(j)
                c_out(j)
        except StopBuild:
            S.emit(nc, st)
            return nc
        S.barrier()
        AR.reset(m_persist)

        if STOP <= 3.9:
            S.emit(nc, st)
            return nc
        HB = 1024
        ynF = AR.alloc([8, HB], BF16); TynF = [T() for _ in range(8)]
        actT = AR.alloc([NCH, HB], BF16); TactT = [T() for _ in range(NCH)]
        wfo = AR.alloc([NCH, 1024], BF16); Twfo = [T() for _ in range(NCH)]
        wfs = [AR.alloc([2, 8, 128], F32) for _ in range(2)]; Twfs = [T(), T()]
        wfb = [AR.alloc([2, 8, 128], BF16) for _ in range(2)]; Twfb = [T(), T()]
        wos = [AR.alloc([1024], F32) for _ in range(2)]; Twos = [T(), T()]
        xd = [AR.alloc([1024], F32) for _ in range(2)]; Txd = [T(), T()]
        xnd = [AR.alloc([1024], BF16) for _ in range(2)]; Txnd = [T(), T()]
        junk2 = AR.alloc([1024], BF16); Tjunk2 = T()
        sgt = [AR.alloc([512], F32) for _ in range(2)]; Tsgt = [T(), T()]
        ot = [AR.alloc([1024], F32) for _ in range(2)]; Tot = [T(), T()]
        dst_ = AR.alloc([NX, 8], F32); Tdst = [T() for _ in range(NX)]
        Tout = T()
        wfi_v = wfi_d.rearrange("(k p) n -> p k n", p=128)

        for hb in range(2):
            for tl in range(8):
                j = hb * 8 + tl
                b = j % 2
                dma(xd[b], x1_d[j * 128:(j + 1) * 128, :], [Tx1d[j]], [Txd[b]])
                sq = dst_[:, j, :]
                act(junk2, xd[b], AF.Square, [Txd[b]], [Tjunk2, Tdst[j]], accum=sq[:, 0:1])
                rstd_ops(sq[:, 1:2], sq[:, 0:1], sq[:, 2:3], 1.0 / D, [Tdst[j]], [Tdst[j]], Tdst[j])
                ts("dve", xnd[b], xd[b], sq[:, 1:2], None, ALU.mult, None, [Txd[b], Tdst[j]], [Txnd[b]])
                for k in range(8):
                    tr(PB0h[:, k * 128:(k + 1) * 128], xnd[b][:, k * 128:(k + 1) * 128], identb, [Txnd[b], Tcb], [TB[0]])
                for k in range(8):
                    act(ynF[:, k, tl * 128:(tl + 1) * 128], PB0h[:, k * 128:(k + 1) * 128], AF.Identity,
                        [TB[0], TGf, TmodT], [TynF[tl]], scale=Gf[:, k, 0:1], bias=modT[:, 24 + k, 0:1])
            for c in range(NCH):
                wb = c % 2
                dma(wfs[wb][:, 0], wfi_v[:, :, c * 128:(c + 1) * 128], [], [Twfs[wb]])
                dma(wfs[wb][:, 1], wfi_v[:, :, FH + c * 128:FH + (c + 1) * 128], [], [Twfs[wb]])
                cp("act", wfb[wb], wfs[wb], [Twfs[wb]], [Twfb[wb]])
                if hb == 0:
                    dma(wos[wb], wfo_d[c * 128:(c + 1) * 128, :], [], [Twos[wb]])
                    cp("act", wfo[:, c, :], wos[wb], [Twos[wb]], [Twfo[c]])
                for tb in range(2):
                    pg = PB[1 + 2 * tb]; pu = PB[2 + 2 * tb]
                    tcs = slice(tb * 512, (tb + 1) * 512)
                    for k in range(8):
                        mm(pg, wfb[wb][:, 0, k, :], ynF[:, k, tcs], k == 0, k == 7, [Twfb[wb]] + TynF[tb * 4:(tb + 1) * 4], [TB[1 + 2 * tb]])
                    for k in range(8):
                        mm(pu, wfb[wb][:, 1, k, :], ynF[:, k, tcs], k == 0, k == 7, [Twfb[wb]] + TynF[tb * 4:(tb + 1) * 4], [TB[2 + 2 * tb]])
                    act(sgt[tb], pg, AF.Silu, [TB[1 + 2 * tb]], [Tsgt[tb]])
                    tt("dve", actT[:, c, tcs], pu, sgt[tb], ALU.mult, [TB[2 + 2 * tb], Tsgt[tb]], [TactT[c]])
            for tl in range(8):
                j = hb * 8 + tl
                b = j % 2
                dma(xd[b], x1_d[j * 128:(j + 1) * 128, :], [Tx1d[j]], [Txd[b]])
                for hf in range(2):
                    for c in range(NCH):
                        mm(PB[5 + hf], actT[:, c, tl * 128:(tl + 1) * 128], wfo[:, c, hf * 512:(hf + 1) * 512],
                           c == 0, c == NCH - 1, [TactT[c], Twfo[c]], [TB[5 + hf]])
                sq = dst_[:, j, :]
                for hf in range(2):
                    act(ot[b][:, hf * 512:(hf + 1) * 512], PB[5 + hf], AF.Square, [TB[5 + hf]], [Tot[b], Tdst[j]],
                        accum=sq[:, 3 + hf:4 + hf])
                tt("dve", sq[:, 5:6], sq[:, 3:4], sq[:, 4:5], ALU.add, [Tdst[j]], [Tdst[j]])
                rstd_ops(sq[:, 6:7], sq[:, 5:6], sq[:, 7:8], 1.0 / D, [Tdst[j]], [Tdst[j]], Tdst[j])
                for hf in range(2):
                    cs = slice(hf * 512, (hf + 1) * 512)
                    stt("dve", ot[b][:, cs], PB[5 + hf], sq[:, 6:7], ggf[:, cs], ALU.mult, ALU.mult,
                        [TB[5 + hf], Tdst[j], Tggf], [Tot[b]])
                tt("dve", ot[b], ot[b], xd[b], ALU.add, [Tot[b], Txd[b]], [Tot[b]])
                dma(out_d[j * 128:(j + 1) * 128, :], ot[b], [Tot[b]], [Tout, Tx1d[j]])

        S.emit(nc, st)
    return nc


def _consts():
    idx = np.arange(128)
    u = idx[:, None]; v = idx[None, :]
    cf = np.zeros((128, 2, 128), np.float32)
    cf[:, 0] = np.where(u <= v, 1.0, 0.0)
    cf[:, 1] = np.where(u >= v, 1.0, 0.0)
    cb = np.zeros((128, 21, 128), np.float32)
    cb[:, 0] = np.eye(128)
    cb[:, 1] = 1.0
    mp = np.where(u >= v, 0.0, NEG)
    mn = np.where(u <= v, 0.0, NEG)
    for i in range(4):
        cb[:, 2 + i] = mp
        cb[:, 6 + i] = mn
        cb[:, 13 + i] = np.where(u <= v, 0.0, NEG)
        cb[:, 17 + i] = np.where(u >= v, 0.0, NEG)
    cb[:, 10] = np.where(u > v, -1.0, 0.0)
    cb[:, 11] = np.where(u < v, -1.0, 0.0)
    cb[:, 12] = -1.0
    cb = cb.astype(ml_dtypes.bfloat16)
    half = 16
    inv_freq = np.power(np.float32(10000.0), -np.arange(half, dtype=np.float32) / np.float32(half)).astype(np.float32)
    t = np.arange(S_LEN)
    row = (t // 64).astype(np.float32); col = (t % 64).astype(np.float32)
    ang_r = row[:, None] * inv_freq; ang_c = col[:, None] * inv_freq
    cos64 = np.concatenate([np.cos(ang_r), np.cos(ang_r), np.cos(ang_c), np.cos(ang_c)], axis=1)
    sin64 = np.concatenate([np.sin(ang_r), np.sin(ang_r), np.sin(ang_c), np.sin(ang_c)], axis=1)
    rope = np.stack([cos64, sin64], axis=1).astype(np.float32)
    rope = rope.reshape(NX, 128, 2, 64).transpose(1, 0, 2, 3).copy()
    return cf, cb, rope


_CACHE = {}


def kernel(x, c, ctx, c_ctx, w_ada, b_ada, g_pre_mix, w_in, w_conv_qk, b_gates, attn_sink,
           g_mlstm_out, w_out, g_post_mix, g_pre_ffn, w_ffn_in, w_ffn_out, g_post_ffn):
    f = lambda a: np.ascontiguousarray(np.asarray(a, dtype=np.float32))
    x = f(x); c = f(c); ctx = f(ctx); c_ctx = f(c_ctx)
    w_ada = f(w_ada)[0]; b_ada = f(b_ada)[0]; w_in = f(w_in)[0]; w_out = f(w_out)[0]
    w_ffn_in = f(w_ffn_in)[0]; w_ffn_out = f(w_ffn_out)[0]
    perm = []
    for i in range(4):
        for g in range(2):
            h = g * 4 + i
            perm += list(range(h * 64, (h + 1) * 64))
    perm += list(range(512, 640))
    perm += list(range(640, 768))
    perm += list(range(2304, 2320))
    perm += list(range(1280, 1792))
    perm += list(range(1792, 2304))
    perm += list(range(768, 1280))
    w_in_p = np.ascontiguousarray(w_in[:, perm])
    rperm = []
    for i in range(4):
        for g in range(2):
            h = g * 4 + i
            rperm += list(range(h * 64, (h + 1) * 64))
    rperm += list(range(512, 1024))
    w_out_p = np.ascontiguousarray(w_out[rperm, :])
    vfm = np.zeros((128, 80), np.float32)
    vfm[:, 32:80] = b_ada.reshape(48, 128).T
    vfm[:, 0:8] = f(g_pre_mix)[0].reshape(8, 128).T
    vfm[:, 8:16] = f(g_pre_ffn)[0].reshape(8, 128).T
    wc = f(w_conv_qk)[0]
    vfm[:, 16:28] = wc.reshape(3, 4, 128).transpose(2, 1, 0).reshape(128, 12)
    sk = f(attn_sink)[0]
    vfm[0:64, 28:32] = sk[0:4][None, :]
    vfm[64:128, 28:32] = sk[4:8][None, :]
    vrow = np.concatenate([f(g_post_mix)[0], f(g_post_ffn)[0], f(g_mlstm_out)[0], f(b_gates)[0]])[None, :]
    vrow = np.ascontiguousarray(vrow)
    cf, cb, rope = _consts()
    if "nc" not in _CACHE:
        _CACHE["nc"] = build_program()
    nc = _CACHE["nc"]
    in_maps = []
    tag = lambda w, b: np.concatenate([w, np.full((1, w.shape[1]), float(b), np.float32)], axis=0)
    for b in range(8):
        ccb = np.stack([c[b].reshape(8, 128).T, c_ctx.reshape(8, 128).T], axis=2).reshape(128, 16)
        in_maps.append({
            "x": x[b], "ctx": ctx[b], "cc": np.ascontiguousarray(ccb),
            "w_ada": tag(w_ada, b), "b_ada": b_ada[None, :], "w_in": tag(w_in_p, b), "w_out": tag(w_out_p, b),
            "w_ffn_in": tag(w_ffn_in, b), "w_ffn_out": tag(w_ffn_out, b), "vec_fm": vfm, "vec_row": vrow,
            "cst_f32": cf, "cst_bf16": cb,
            "rope": np.concatenate([rope.reshape(128, NX * 128), np.full((128, 1), float(b), np.float32)], axis=1),
        })
    res = run_bass_kernel_spmd(nc, in_maps, core_ids=list(range(8)))
    out = np.stack([np.asarray(r["out"], dtype=np.float32) for r in res.results], axis=0)
    if DEBUG:
        kernel.dbg = [dict(r) for r in res.results]
    return out
```

```python
import os
import numpy as np
import ml_dtypes
from contextlib import ExitStack
import concourse.bass as bass
import concourse.mybir as mybir
from concourse.bass_utils import run_bass_kernel_spmd

F32 = mybir.dt.float32
BF16 = mybir.dt.bfloat16
ALU = mybir.AluOpType
AF = mybir.ActivationFunctionType

D = 1024
S_LEN = 2048
L_CTX = 256
NT = 18
NX = 16
FH = 2816
NCH = 22
EPS = 1e-6
NEG = -30000.0
DEBUG = bool(int(os.environ.get('MK_DEBUG', '0')))
STOP = float(os.environ.get('MK_STOP', '99'))
SUB = int(os.environ.get('MK_SUB', '9'))


class StopBuild(Exception):
    pass


class T:
    def __init__(self, name=""):
        self.name = name
        self.w = None
        self.r = []


class Op:
    __slots__ = ("eng", "idx", "fn", "deps", "raw", "inc", "cnt", "dma", "dsem", "dval", "gidx", "cost", "nbytes", "seg", "fin")
    _G = [0]

    def __init__(self, eng, idx, fn, dma):
        Op._G[0] += 1
        self.gidx = Op._G[0]
        self.eng = eng; self.idx = idx; self.fn = fn; self.deps = set(); self.raw = set(); self.inc = False
        self.cnt = 0; self.dma = dma; self.dsem = None; self.dval = 0
        self.cost = 0.1; self.nbytes = 0; self.seg = 0; self.fin = 0.0


class Sched:
    ENG = ["pe", "act", "dve", "pool", "sp"]
    NDMA = 8

    def __init__(self):
        self.ops = {e: [] for e in self.ENG}
        self.bar = []
        self.seg = 0

    def add(self, eng, fn, reads=(), writes=(), dma=False, cost=0.1, nbytes=0):
        op = Op(eng, len(self.ops[eng]), fn, dma)
        op.cost = cost; op.nbytes = nbytes; op.seg = self.seg
        for t in reads:
            if t.w is not None:
                op.deps.add(t.w)
                op.raw.add(t.w)
        for t in writes:
            if t.w is not None:
                op.deps.add(t.w)
            for r in t.r:
                op.deps.add(r)
        for t in reads:
            if not dma:
                keep = []
                for r in t.r:
                    if r.dma or r.eng != eng:
                        keep.append(r)
                    else:
                        op.deps.add(r)
                t.r = keep
            t.r.append(op)
        for t in writes:
            t.w = op
            t.r = []
        op.deps.discard(op)
        self.ops[eng].append(op)
        return op

    def barrier(self):
        self.seg += 1

    def schedule(self):
        BW = 200e3
        final = {e: [] for e in self.ENG}
        nseg = self.seg + 1
        free = {e: 0.0 for e in self.ENG}
        dma_free = 0.0
        prev_last = []
        W = int(os.environ.get("MK_W", "40"))
        FIXED = set(os.environ.get("MK_FIX", "dve").split(","))
        for sg in range(nseg):
            lists = {e: [o for o in self.ops[e] if o.seg == sg] for e in self.ENG}
            ptr = {e: 0 for e in self.ENG}
            done = set()
            segstart = max(free.values()) if sg > 0 else 0.0
            for e in self.ENG:
                free[e] = max(free[e], segstart)
            remaining = sum(len(v) for v in lists.values())
            sched_flag = {}
            while remaining:
                best = None
                for e in self.ENG:
                    lst = lists[e]
                    i = ptr[e]
                    cnt = 0
                    while i < len(lst) and cnt < (1 if e in FIXED else W):
                        o = lst[i]
                        i += 1
                        if id(o) in sched_flag:
                            continue
                        cnt += 1
                        ok = True
                        rdy = free[e]
                        for d in o.deps:
                            if d.seg != sg:
                                continue
                            if id(d) not in sched_flag:
                                ok = False
                                break
                            if d.fin > rdy:
                                rdy = d.fin
                        if not ok:
                            continue
                        key = (rdy, o.gidx)
                        if best is None or key < best[0]:
                            best = (key, e, o)
                        if rdy <= free[e]:
                            break
                assert best is not None, "scheduler stuck"
                (rdy, _), e, o = best
                if o.dma:
                    issue_end = rdy + 0.1
                    st_bw = max(issue_end, dma_free)
                    dma_free = st_bw + o.nbytes / BW
                    o.fin = dma_free + 2.0
                    free[e] = issue_end
                else:
                    o.fin = rdy + o.cost
                    free[e] = o.fin
                sched_flag[id(o)] = True
                final[e].append(o)
                remaining -= 1
                lst = lists[e]
                while ptr[e] < len(lst) and id(lst[ptr[e]]) in sched_flag:
                    ptr[e] += 1
            for e in self.ENG:
                free[e] = max([free[e]] + [o.fin for o in final[e][-1:]])
            prev_last = [final[e][-1] for e in self.ENG if final[e]]
            if os.environ.get("MK_STATS"):
                print("SEG", sg, "ends at us", max(free.values()), {e: round(sum(o.cost for o in lists[e]), 1) for e in self.ENG})
            if sg + 1 < nseg:
                for e in self.ENG:
                    for o in self.ops[e]:
                        if o.seg == sg + 1:
                            o.deps.update(prev_last)
        for e in self.ENG:
            self.ops[e] = final[e]
            for i, o in enumerate(final[e]):
                o.idx = i
        if os.environ.get("MK_STATS"):
            print("SCHED est makespan us", max(free.values()))

    def emit(self, nc, stack):
        if os.environ.get("MK_NOSCHED") is None:
            self.schedule()
        else:
            bars = {}
            for sg in range(self.seg):
                bars[sg + 1] = [[o for o in self.ops[e] if o.seg <= sg][-1] for e in self.ENG if [o for o in self.ops[e] if o.seg <= sg]]
            for e in self.ENG:
                for o in self.ops[e]:
                    if o.seg in bars:
                        o.deps.update(bars[o.seg])
        need = {}
        for e in self.ENG:
            for op in self.ops[e]:
                nd = []
                for d in op.deps:
                    if d.dma or d.eng != op.eng:
                        nd.append(d)
                    else:
                        if op.eng == "pe" and not op.dma:
                            continue
                        if op.dma or (d in op.raw and op.idx - d.idx <= 2):
                            nd.append(d)
                for d in nd:
                    d.inc = True
                need[op] = nd
        sems = {e: stack.enter_context(nc.semaphore("s_" + e)) for e in self.ENG}
        dsems = {e: [stack.enter_context(nc.semaphore("d_%s%d" % (e, i))) for i in range(self.NDMA)]
                 for e in ("sp", "pool")}
        for e in self.ENG:
            c = 0
            nd = 0
            for op in self.ops[e]:
                if op.dma:
                    op.dsem = (e, nd % self.NDMA)
                    op.dval = 16 * (nd // self.NDMA + 1)
                    nd += 1
                elif op.inc:
                    c += 1
                    op.cnt = c
        if os.environ.get("MK_STATS"):
            for e in self.ENG:
                print("ENG", e, "ops", len(self.ops[e]), "incs", sum(1 for o in self.ops[e] if o.inc),
                      "waits", sum(len(need[o]) for o in self.ops[e]))
        block = stack.enter_context(nc.Block())

        def run(e):
            def body(eng):
                waited = {}
                dmas = []
                for op in self.ops[e]:
                    pend = []
                    for d in sorted(need[op], key=lambda o: o.gidx):
                        if d.dma:
                            key = d.dsem; val = d.dval; sem = dsems[d.dsem[0]][d.dsem[1]]
                        else:
                            key = d.eng; val = d.cnt; sem = sems[d.eng]
                        if waited.get(key, 0) >= val:
                            continue
                        waited[key] = val
                        pend.append((sem, val))
                    last = None
                    if pend and not op.dma:
                        last = pend.pop()
                    for (sem, val) in pend:
                        eng.wait_ge(sem, val)
                    if op.dma:
                        k = len(dmas)
                        if k >= self.NDMA:
                            prev = dmas[k - self.NDMA]
                            if waited.get(prev.dsem, 0) < prev.dval:
                                waited[prev.dsem] = prev.dval
                                eng.wait_ge(dsems[prev.dsem[0]][prev.dsem[1]], prev.dval)
                        dmas.append(op)
                        op.fn(eng).then_inc(dsems[op.dsem[0]][op.dsem[1]], 16)
                    else:
                        ins = op.fn(eng)
                        if last is not None:
                            ins._wait_ge(last[0], last[1])
                        if op.inc:
                            ins.then_inc(sems[e], 1)
                for op in dmas[-self.NDMA:]:
                    if waited.get(op.dsem, 0) < op.dval:
                        waited[op.dsem] = op.dval
                        eng.wait_ge(dsems[op.dsem[0]][op.dsem[1]], op.dval)
            return body

        block.tensor(run("pe"))
        block.scalar(run("act"))
        block.vector(run("dve"))
        block.gpsimd(run("pool"))
        block.sync(run("sp"))


class Arena:
    def __init__(self, t, nwords):
        self.t = t; self.n = nwords; self.off = 0

    def mark(self):
        return self.off

    def reset(self, m):
        self.off = m

    def alloc(self, free_shape, dt):
        n_el = 1
        for s in free_shape:
            n_el *= s
        esz = 4 if dt == F32 else 2
        nw = (n_el * esz + 3) // 4
        nw = (nw + 7) // 8 * 8
        assert self.off + nw <= self.n, ("arena overflow", self.off, nw, self.n)
        ap = self.t[:, self.off:self.off + nw]
        self.off += nw
        if dt != F32:
            ap = ap.bitcast(dt)
        ap = ap[:, 0:n_el]
        if len(free_shape) == 2:
            ap = ap.rearrange("p (a b) -> p a b", a=free_shape[0])
        elif len(free_shape) == 3:
            ap = ap.rearrange("p (a b c) -> p a b c", a=free_shape[0], b=free_shape[1])
        elif len(free_shape) == 4:
            ap = ap.rearrange("p (a b c d) -> p a b c d", a=free_shape[0], b=free_shape[1], c=free_shape[2])
        return ap


def build_program():
    nc = bass.Bass("TRN2", target_bir_lowering=False)
    dr = lambda name, shape, dt=F32, kind="ExternalInput": nc.dram_tensor(name, shape, dt, kind=kind).ap()
    x_d = dr("x", [S_LEN, D]); ctx_d = dr("ctx", [L_CTX, D]); cc_d = dr("cc", [128, 16])
    wada_d = dr("w_ada", [D + 1, 6 * D])[0:D, :]; bada_d = dr("b_ada", [1, 6 * D])
    win_d = dr("w_in", [D + 1, 2320])[0:D, :]; wout_d = dr("w_out", [D + 1, D])[0:D, :]
    wfi_d = dr("w_ffn_in", [D + 1, 2 * FH])[0:D, :]; wfo_d = dr("w_ffn_out", [FH + 1, D])[0:FH, :]
    vfm_d = dr("vec_fm", [128, 80])
    vrow_d = dr("vec_row", [1, 2576])
    cf_d = dr("cst_f32", [128, 2, 128])
    cb_d = dr("cst_bf16", [128, 21, 128], BF16)
    rope_d = dr("rope", [128, NX * 128 + 1])[:, 0:NX * 128].rearrange("p (j c d) -> p j c d", j=NX, c=2)
    out_d = dr("out", [S_LEN, D], F32, "ExternalOutput")
    x1_d = dr("x1s", [S_LEN, D], F32, "ExternalOutput")

    S = Sched()
    st = ExitStack()
    with st:
        arena_t = st.enter_context(nc.sbuf_tensor("arena", [128, 53000], F32))
        AR = Arena(arena_t, 53000)
        psum_t = st.enter_context(nc.psum_tensor("psum", [128, 8, 512], F32))
        PB = [psum_t[:, i, :] for i in range(8)]
        TB = [T("bank%d" % i) for i in range(8)]
        PB0h = PB[0].bitcast(BF16)

        def fsz(ap):
            n = 1
            for d_ in ap.shape[1:]:
                n *= d_
            return n

        def c_act(ap):
            return (224.0 + fsz(ap)) / 1400.0

        def c_dve(ap):
            return (64.0 + fsz(ap)) / 960.0

        def dma(out, in_, R, W, q="sp"):
            nb = out.shape[0] * fsz(out) * (4 if out.dtype == F32 else 2)
            S.add(q, lambda e: e.dma_start(out=out, in_=in_), reads=R, writes=W, dma=True, nbytes=nb)

        def mm(out, lhsT, rhs, start, stop, R, W):
            S.add("pe", lambda e: e.matmul(out, lhsT=lhsT, rhs=rhs, start=start, stop=stop), reads=R, writes=W,
                  cost=max(fsz(rhs), 64) / 2000.0 + 0.04)

        def tr(out, in_, ident, R, W):
            S.add("pe", lambda e: e.transpose(out=out, in_=in_, identity=ident), reads=R, writes=W, cost=0.1)

        def act(out, in_, func, R, W, scale=None, bias=None, accum=None):
            kw = {}
            if scale is not None:
                kw["scale"] = scale
            if bias is not None:
                kw["bias"] = bias
            if accum is not None:
                kw["accum_out"] = accum
            S.add("act", lambda e: e.activation(out=out, in_=in_, func=func, **kw), reads=R, writes=W, cost=c_act(out))

        def tt(eng, out, in0, in1, op, R, W):
            S.add(eng, lambda e: e.tensor_tensor(out=out, in0=in0, in1=in1, op=op), reads=R, writes=W, cost=c_dve(out))

        def ts(eng, out, in0, s1, s2, op0, op1, R, W):
            if s2 is None:
                S.add(eng, lambda e: e.tensor_scalar(out=out, in0=in0, scalar1=s1, scalar2=None, op0=op0), reads=R, writes=W, cost=c_dve(out))
            else:
                S.add(eng, lambda e: e.tensor_scalar(out=out, in0=in0, scalar1=s1, scalar2=s2, op0=op0, op1=op1), reads=R, writes=W, cost=c_dve(out))

        def stt(eng, out, in0, scalar, in1, op0, op1, R, W):
            S.add(eng, lambda e: e.scalar_tensor_tensor(out=out, in0=in0, scalar=scalar, in1=in1, op0=op0, op1=op1),
                  reads=R, writes=W, cost=c_dve(out))

        def cp(eng, out, in_, R, W):
            if eng == "act":
                act(out, in_, AF.Copy, R, W)
            else:
                S.add(eng, lambda e: e.tensor_copy(out=out, in_=in_), reads=R, writes=W, cost=c_dve(out))

        def recip(out, in_, R, W):
            S.add("dve", lambda e: e.reciprocal(out=out, in_=in_), reads=R, writes=W, cost=c_dve(out))

        def memset(eng, ap, val, W):
            S.add(eng, lambda e: e.memset(ap, val), writes=W, cost=c_dve(ap))

        def bc(ap, shape):
            return ap.to_broadcast(shape)

        def rstd_ops(out, ssq, tmp, inv_n, R, W, Tt):
            act(tmp, ssq, AF.Ln, R + [Teps], [Tt], scale=inv_n, bias=epsb)
            act(out, tmp, AF.Exp, [Tt], W, scale=-0.5)

        def dump(name, ap, R):
            if not DEBUG or name not in os.environ.get('MK_DUMP', '').split(','):
                return
            dd = nc.dram_tensor("dbg_" + name, list(ap.shape), ap.dtype, kind="ExternalOutput").ap()
            dma(dd, ap, R, [T()])

        epsb = AR.alloc([1], F32); Teps = T()
        memset("dve", epsb, EPS, [Teps])
        cf = AR.alloc([2, 128], F32); Tcf = T()
        cb = AR.alloc([21, 128], BF16); Tcb = T()
        tri_f = cf[:, 0, :]; tri_b = cf[:, 1, :]
        identb = cb[:, 0, :]; onesb = cb[:, 1, :]
        mask_prev = cb[:, 2:6, :]; mask_next = cb[:, 6:10, :]
        nstr_f = cb[:, 10, :]; nstr_b = cb[:, 11, :]; negones = cb[:, 12, :]
        madd_f4 = cb[:, 13:17, :]; madd_b4 = cb[:, 17:21, :]
        vfm = AR.alloc([80], F32); Tvfm = T()
        badaT = vfm[:, 32:80]
        gpre_m = vfm[:, 0:8]; gpre_f = vfm[:, 8:16]; wconv = vfm[:, 16:28].rearrange("p (c k) -> p c k", c=4)
        esink = AR.alloc([4], F32); Tes = T()
        cc = AR.alloc([8, 2], F32); Tcc = T()
        scc = AR.alloc([8, 2], F32); Tscc = T()
        sccb = AR.alloc([8, 2], BF16); Tsccb = T()
        sccx = AR.alloc([8, 128], BF16); Tsccx = T()
        modT = AR.alloc([48, 2], F32); TmodT = T()
        Gm = AR.alloc([8, 2], F32); TGm = T()
        Gf = AR.alloc([8, 2], F32); TGf = T()
        ggm = AR.alloc([1024], F32); Tggm = T()
        ggf = AR.alloc([1024], F32); Tggf = T()
        gain_bc = AR.alloc([512], F32); Tgain = T()
        bg_bc = AR.alloc([16], F32); Tbg = T()
        stat = AR.alloc([NT, 4], F32)
        Tstat = [T() for _ in range(NT)]

        dma(cf, cf_d, [], [Tcf]); dma(cb, cb_d, [], [Tcb]); dma(vfm, vfm_d, [], [Tvfm])
        dma(cc, cc_d.rearrange("p (k t) -> p k t", k=8), [], [Tcc])
        act(scc, cc, AF.Silu, [Tcc], [Tscc])
        cp("dve", sccb, scc, [Tscc], [Tsccb])
        cp("dve", sccx, bc(scc[:, :, 0:1], [128, 8, 128]), [Tscc], [Tsccx])
        act(esink, vfm[:, 28:32], AF.Exp, [Tvfm], [Tes])

        m_persist = AR.mark()
        qT = AR.alloc([4, S_LEN], BF16); TqT = [T() for _ in range(NX)]
        kT = AR.alloc([S_LEN], BF16); TkT = [T() for _ in range(NX)]
        kcT = AR.alloc([L_CTX], BF16); TkcT = [T() for _ in range(2)]
        Vst = AR.alloc([NT, 128], BF16); TV = [T() for _ in range(NT)]
        mqk = AR.alloc([4, NT * 128], BF16); Tmqk = [T() for _ in range(NT)]
        mKt = AR.alloc([NT, 256], BF16); TmKt = [T() for _ in range(NT)]
        mVa = AR.alloc([NT, 4, 129], BF16); TmVa = [T() for _ in range(NT)]
        gsig = AR.alloc([NX, 512], BF16); Tgsig = [T() for _ in range(NX)]
        gts = AR.alloc([NT, 16], F32); Tgts = [T() for _ in range(NT)]
        nlf = AR.alloc([NT, 2, 4], F32); Tnlf = T()
        nlfh = AR.alloc([NT, 2, 4], BF16); nlfl = AR.alloc([NT, 2, 4], BF16); Tnlfs = T()
        gam = AR.alloc([NT, 2, 4], F32); Tgam = T()
        rope = AR.alloc([NX, 2, 64], F32); Trope = T()
        dma(rope, rope_d, [], [Trope])
        TmVa_all = T()
        S.add("dve", lambda e: e.memset(mVa, 1.0), writes=TmVa + [TmVa_all])

        m_region = AR.mark()

        stg = [AR.alloc([1160], F32) for _ in range(2)]; Tstg = [T(), T()]
        win = AR.alloc([8, 2320], BF16); Twin = [T() for _ in range(8)]
        rowb = AR.alloc([1024], F32); Trowb = T()
        wab = [AR.alloc([1024], BF16) for _ in range(2)]; Twab = [T(), T()]
        xin = [AR.alloc([1024], F32) for _ in range(2)]; Txin = [T(), T()]
        junk = AR.alloc([1024], BF16); Tjunk = T()
        xn = [AR.alloc([1024], BF16) for _ in range(2)]; Txn = [T(), T()]
        ynT = [AR.alloc([8, 128], BF16) for _ in range(2)]; TynT = [T(), T()]
        ropeA = AR.alloc([640], F32); TrA = T()
        ropeB = AR.alloc([640], F32); TrB = T()
        qtok = AR.alloc([640], BF16); Tqtok = T()
        raw = AR.alloc([3, 4, 130], F32); Traw = [T(), T(), T()]
        cacc = AR.alloc([4, 128], F32); Tcacc = T()
        ctmp = AR.alloc([4, 128], F32); Tctmp = T()
        gtmp = AR.alloc([NT, 2, 4], F32); Tgtmp = T()

        def ada_load(seg, k):
            b = k % 2
            dma(stg[b][:, 0:1024], wada_d[k * 128:(k + 1) * 128, seg * 1024:(seg + 1) * 1024], [], [Tstg[b]])
            cp("act", wab[b], stg[b][:, 0:1024], [Tstg[b]], [Twab[b]])
            return b

        def ada_segment(seg):
            wada_v = wada_d.rearrange("(k p) n -> p k n", p=128)
            for c8 in range(8):
                b = c8 % 2
                col0 = seg * 1024 + c8 * 128
                dma(stg[b][:, 0:1024].rearrange("p (k n) -> p k n", k=8), wada_v[:, :, col0:col0 + 128], [], [Tstg[b]])
                cp("act", wab[b], stg[b][:, 0:1024], [Tstg[b]], [Twab[b]])
                for k in range(8):
                    mm(PB[5][:, c8 * 2:(c8 + 1) * 2], wab[b][:, k * 128:(k + 1) * 128], sccb[:, k, :], k == 0, k == 7,
                       [Tsccb, Twab[b]], [TB[5]])
            tt("dve", modT[:, seg * 8:(seg + 1) * 8, :], PB[5][:, 0:16].rearrange("p (a b) -> p a b", a=8),
               bc(badaT[:, seg * 8:(seg + 1) * 8].unsqueeze(2), [128, 8, 2]), ALU.add, [TB[5], Tvfm], [TmodT])

        def ada_row(seg, dst, Tdst_):
            dma(rowb, bada_d[0:1, seg * 1024:(seg + 1) * 1024].to_broadcast([128, 1024]), [], [Trowb])
            for k in range(8):
                b = ada_load(seg, k)
                for hf in range(2):
                    mm(PB[6 + hf], sccx[:, k, :], wab[b][:, hf * 512:(hf + 1) * 512], k == 0, k == 7,
                       [Tsccx, Twab[b]], [TB[6 + hf]])
            for hf in range(2):
                cs = slice(hf * 512, (hf + 1) * 512)
                tt("dve", rowb[:, cs], PB[6 + hf], rowb[:, cs], ALU.add, [TB[6 + hf], Trowb], [Trowb])
                tt("dve", dst[:, cs], dst[:, cs], rowb[:, cs], ALU.mult, [Trowb, Tdst_], [Tdst_])

        ada_segment(0)
        ada_segment(1)
        stt("dve", Gm, modT[:, 8:16, :], 1.0, bc(gpre_m.unsqueeze(2), [128, 8, 2]), ALU.add, ALU.mult, [TmodT, Tvfm], [TGm])
        dma(gain_bc, vrow_d[0:1, 2048:2560].to_broadcast([128, 512]), [], [Tgain])
        dma(bg_bc, vrow_d[0:1, 2560:2576].to_broadcast([128, 16]), [], [Tbg])
        dma(ggm, vrow_d[0:1, 0:1024].to_broadcast([128, 1024]), [], [Tggm])
        dma(ggf, vrow_d[0:1, 1024:2048].to_broadcast([128, 1024]), [], [Tggf])

        if STOP <= 0:
            S.emit(nc, st)
            return nc
        for k in range(8):
            for hh in range(2):
                dma(stg[hh], win_d[k * 128:(k + 1) * 128, hh * 1160:(hh + 1) * 1160], [], [Tstg[hh]])
                cp("act", win[:, k, hh * 1160:(hh + 1) * 1160], stg[hh], [Tstg[hh]], [Twin[k]])

        def conv_tile(tau):
            sl = tau % 3
            rw = raw[:, sl]
            w0 = bc(wconv[:, :, 0:1], [128, 4, 128]); w1 = bc(wconv[:, :, 1:2], [128, 4, 128]); w2 = bc(wconv[:, :, 2:3], [128, 4, 128])
            tt("dve", cacc, rw[:, :, 1:129], w1, ALU.mult, [Traw[sl], Tvfm], [Tcacc])
            tt("dve", ctmp, rw[:, :, 0:128], w0, ALU.mult, [Traw[sl], Tvfm], [Tctmp])
            tt("dve", cacc, cacc, ctmp, ALU.add, [Tcacc, Tctmp], [Tcacc])
            tt("dve", ctmp, rw[:, :, 2:130], w2, ALU.mult, [Traw[sl], Tvfm], [Tctmp])
            tt("dve", cacc, cacc, ctmp, ALU.add, [Tcacc, Tctmp], [Tcacc])
            act(mqk[:, :, tau * 128:(tau + 1) * 128], cacc, AF.Silu, [Tcacc], [Tmqk[tau]])
            for pr in range(2):
                tr(PB0h[:, pr * 128:(pr + 1) * 128], mqk[:, 2 + pr, tau * 128:(tau + 1) * 128], identb, [Tmqk[tau], Tcb], [TB[0]])
            cp("act", mKt[:, tau, :], PB0h[:, 0:256], [TB[0]], [TmKt[tau]])

        def phaseA_front(tau):
            is_ctx = tau < 2
            b = tau % 2
            src = ctx_d[tau * 128:(tau + 1) * 128, :] if is_ctx else x_d[(tau - 2) * 128:(tau - 1) * 128, :]
            dma(xin[b], src, [], [Txin[b]])
            sq = stat[:, tau, :]
            act(junk, xin[b], AF.Square, [Txin[b]], [Tjunk, Tstat[tau]], accum=sq[:, 0:1])
            rstd_ops(sq[:, 1:2], sq[:, 0:1], sq[:, 2:3], 1.0 / D, [Tstat[tau]], [Tstat[tau]], Tstat[tau])
            ts("dve", xn[b], xin[b], sq[:, 1:2], None, ALU.mult, None, [Txin[b], Tstat[tau]], [Txn[b]])
            for k in range(8):
                tr(PB0h[:, k * 128:(k + 1) * 128], xn[b][:, k * 128:(k + 1) * 128], identb, [Txn[b], Tcb], [TB[0]])
            mi = 1 if is_ctx else 0
            for k in range(8):
                act(ynT[b][:, k, :], PB0h[:, k * 128:(k + 1) * 128], AF.Identity, [TB[0], TGm, TmodT], [TynT[b]],
                    scale=Gm[:, k, mi:mi + 1], bias=modT[:, k, mi:mi + 1])

        def phaseA_rest(tau):
            is_ctx = tau < 2
            b = tau % 2
            if not is_ctx:
                for k in range(8):
                    mm(PB[1], ynT[b][:, k, :], win[:, k, 0:512], k == 0, k == 7, [TynT[b], Twin[k]], [TB[1]])
            for k in range(8):
                mm(PB[2][:, 0:272], ynT[b][:, k, :], win[:, k, 512:784], k == 0, k == 7, [TynT[b], Twin[k]], [TB[2]])
            for k in range(8):
                mm(PB[3], ynT[b][:, k, :], win[:, k, 784:1296], k == 0, k == 7, [TynT[b], Twin[k]], [TB[3]])
            if not is_ctx:
                for k in range(8):
                    mm(PB[4], ynT[b][:, k, :], win[:, k, 1296:1808], k == 0, k == 7, [TynT[b], Twin[k]], [TB[4]])
            for c in range(4):
                for k in range(8):
                    mm(PB[5][:, c * 128:(c + 1) * 128], win[:, k, 1808 + c * 128:1808 + (c + 1) * 128], ynT[b][:, k, :],
                       k == 0, k == 7, [TynT[b], Twin[k]], [TB[5]])
            cp("act", Vst[:, tau, :], PB[2][:, 128:256], [TB[2]], [TV[tau]])
            tt("dve", gts[:, tau, :], PB[2][:, 256:272], bg_bc, ALU.add, [TB[2], Tbg], [Tgts[tau]])
            cp("act", mVa[:, tau, :, 0:128], PB[3].rearrange("p (h v) -> p h v", h=4), [TB[3], TmVa_all], [TmVa[tau]])
            sl = tau % 3
            cp("dve", raw[:, sl, :, 1:129], PB[5].rearrange("p (c t) -> p c t", c=4), [TB[5]], [Traw[sl]])
            first = tau in (0, 2)
            last = tau in (1, NT - 1)
            if first:
                memset("dve", raw[:, sl, :, 0:1], 0.0, [Traw[sl]])
            else:
                pl = (tau - 1) % 3
                cp("dve", raw[:, sl, :, 0:1], raw[:, pl, :, 128:129], [Traw[pl], Traw[sl]], [Traw[sl]])
                cp("dve", raw[:, pl, :, 129:130], raw[:, sl, :, 1:2], [Traw[sl], Traw[pl]], [Traw[pl]])
                conv_tile(tau - 1)
            if last:
                memset("dve", raw[:, sl, :, 129:130], 0.0, [Traw[sl]])
                conv_tile(tau)
            if is_ctx:
                cp("dve", qtok[:, 512:640], PB[2][:, 0:128], [TB[2]], [Tqtok])
                tr(PB0h[:, 0:128], qtok[:, 512:640], identb, [Tqtok, Tcb], [TB[0]])
                cp("act", kcT[:, tau * 128:(tau + 1) * 128], PB0h[:, 0:128], [TB[0]], [TkcT[tau]])
            else:
                j = tau - 2
                act(junk[:, 0:512], PB[4], AF.Sigmoid, [TB[4]], [Tjunk])
                tt("dve", gsig[:, j, :], junk[:, 0:512], gain_bc, ALU.mult, [Tjunk, Tgain], [Tgsig[j]])
                cosb8 = bc(rope[:, j, 0:1, :], [128, 8, 64]); sinb8 = bc(rope[:, j, 1:2, :], [128, 8, 64])
                cosb2 = bc(rope[:, j, 0:1, :], [128, 2, 64]); sinb2 = bc(rope[:, j, 1:2, :], [128, 2, 64])
                A3 = ropeA.rearrange("p (h d) -> p h d", h=10); B3 = ropeB.rearrange("p (h d) -> p h d", h=10)
                tt("dve", A3[:, 0:8, :], PB[1].rearrange("p (h d) -> p h d", h=8), cosb8, ALU.mult, [TB[1], Trope], [TrA])
                tt("dve", B3[:, 0:8, :], PB[1].rearrange("p (h d) -> p h d", h=8), sinb8, ALU.mult, [TB[1], Trope], [TrB])
                tt("dve", A3[:, 8:10, :], PB[2][:, 0:128].rearrange("p (h d) -> p h d", h=2), cosb2, ALU.mult, [TB[2], Trope], [TrA])
                tt("dve", B3[:, 8:10, :], PB[2][:, 0:128].rearrange("p (h d) -> p h d", h=2), sinb2, ALU.mult, [TB[2], Trope], [TrB])
                A4 = ropeA.rearrange("p (g h d) -> p g h d", g=20, h=2); B4 = ropeB.rearrange("p (g h d) -> p g h d", g=20, h=2)
                Q4 = qtok.rearrange("p (g h d) -> p g h d", g=20, h=2)
                tt("dve", Q4[:, :, 0, :], A4[:, :, 0, :], B4[:, :, 1, :], ALU.subtract, [TrA, TrB], [Tqtok])
                tt("dve", Q4[:, :, 1, :], A4[:, :, 1, :], B4[:, :, 0, :], ALU.add, [TrA, TrB], [Tqtok])
                for i in range(5):
                    tr(PB0h[:, i * 128:(i + 1) * 128], qtok[:, i * 128:(i + 1) * 128], identb, [Tqtok, Tcb], [TB[0]])
                cp("act", qT[:, :, j * 128:(j + 1) * 128], PB0h[:, 0:512].rearrange("p (i t) -> p i t", i=4), [TB[0]], [TqT[j]])
                cp("dve", kT[:, j * 128:(j + 1) * 128], PB0h[:, 512:640], [TB[0]], [TkT[j]])
            if tau == 3:
                ada_row(2, ggm, Tggm)
            if tau == 6:
                ada_segment(3)
            if tau == 8:
                ada_segment(4)
                stt("dve", Gf, modT[:, 32:40, :], 1.0, bc(gpre_f.unsqueeze(2), [128, 8, 2]), ALU.add, ALU.mult,
                    [TmodT, Tvfm], [TGf])
            if tau == 10:
                ada_row(5, ggf, Tggf)


        phaseA_front(0)
        for tau in range(NT):
            if tau + 1 < NT:
                phaseA_front(tau + 1)
            phaseA_rest(tau)

        if STOP <= 1:
            S.emit(nc, st)
            return nc
        g5 = gts.rearrange("p n (d i h) -> p n d i h", d=2, i=2)
        Tg_all = Tgts
        act(gtmp, g5[:, :, :, 1, :], AF.Exp, Tg_all, [Tgtmp], scale=-1.0)
        ts("dve", gtmp, gtmp, 1.0, None, ALU.add, None, [Tgtmp], [Tgtmp])
        act(nlf, gtmp, AF.Ln, [Tgtmp], [Tnlf])
        cp("dve", nlfh, nlf, [Tnlf], [Tnlfs])
        tt("dve", nlfl, nlf, nlfh, ALU.subtract, [Tnlf, Tnlfs], [Tnlfs])
        mm(PB[7][:, 0:NT * 8], negones, nlfh.rearrange("p n d h -> p (n d h)"), True, False, [Tnlfs, Tcb], [TB[7]])
        mm(PB[7][:, 0:NT * 8], negones, nlfl.rearrange("p n d h -> p (n d h)"), False, True, [Tnlfs, Tcb], [TB[7]])
        act(gam.rearrange("p n d h -> p (n d h)"), PB[7][:, 0:NT * 8], AF.Exp, [TB[7]], [Tgam])

        dump("modT", modT, [TmodT]); dump("Gm", Gm, [TGm]); dump("ggm", ggm, [Tggm]); dump("gts", gts, Tgts)
        dump("qT", qT[:, :, 0:256], TqT); dump("kT", kT[:, 0:256], TkT); dump("kcT", kcT, TkcT); dump("Vst", Vst[:, 0:4], TV)
        dump("mqk", mqk[:, :, 0:512], Tmqk); dump("mKt", mKt[:, 0:4], TmKt); dump("mVa", mVa[:, 0:4], TmVa); dump("gsig", gsig[:, 0:2], Tgsig)
        dump("nlf", nlf, [Tnlf]); dump("gam", gam, [Tgam])
        if os.environ.get('MK_STATS'):
            print('ARENA end phase A', AR.off, 'persist', m_persist, 'region', m_region)
        S.barrier()
        AR.reset(m_region)

        Sst = AR.alloc([2, NX, 2, 129], BF16); TSst = [[T() for _ in range(NX)] for _ in range(2)]
        wout = AR.alloc([8, 1024], BF16); Twout = [T() for _ in range(8)]
        stg2 = [AR.alloc([1024], F32) for _ in range(2)]; Tstg2 = [T(), T()]
        Cst = AR.alloc([2, 129], F32); TCst = T()
        wk = AR.alloc([4], F32); Twk = T()
        Vw = AR.alloc([4, 129], BF16); TVw = T()
        for k in range(8):
            b = k % 2
            dma(stg2[b], wout_d[k * 128:(k + 1) * 128, :], [], [Tstg2[b]])
            cp("act", wout[:, k, :], stg2[b], [Tstg2[b]], [Twout[k]])

        def scan_step(tau, d, store_j):
            nstr = nstr_f if d == 0 else nstr_b
            mm(PB[4][:, 0:4], nstr, nlfh[:, tau, d, :], True, False, [Tnlfs, Tcb], [TB[4]])
            mm(PB[4][:, 0:4], nstr, nlfl[:, tau, d, :], False, True, [Tnlfs, Tcb], [TB[4]])
            tt("dve", wk, PB[4][:, 0:4], g5[:, tau, d, 0, :], ALU.add, [TB[4], Tgts[tau]], [Twk])
            act(wk, wk, AF.Exp, [Twk], [Twk])
            tt("dve", Vw, mVa[:, tau], bc(wk.unsqueeze(2), [128, 4, 129]), ALU.mult, [TmVa[tau], Twk], [TVw])
            for pr in range(2):
                for eo in range(2):
                    mm(PB[pr][:, eo * 129:(eo + 1) * 129], mKt[:, tau, pr * 128:(pr + 1) * 128], Vw[:, 2 * pr + eo, :],
                       True, True, [TmKt[tau], TVw], [TB[pr]])
            for hp in range(2):
                ps_ = slice(hp * 64, (hp + 1) * 64)
                for pr in range(2):
                    h = 2 * pr + hp
                    stt("dve", Cst[ps_, pr, :], Cst[ps_, pr, :], gam[ps_, tau, d, h:h + 1], PB[pr][ps_, hp * 129:(hp + 1) * 129],
                        ALU.mult, ALU.add, [TCst, Tgam, TB[pr]], [TCst])
            if store_j is not None:
                cp("act", Sst[:, d, store_j], Cst, [TCst], [TSst[d][store_j]])

        memset("dve", Cst, 0.0, [TCst])
        for tau in range(0, NT - 1):
            scan_step(tau, 0, tau - 1 if tau >= 1 else None)
        memset("dve", Cst, 0.0, [TCst])
        for tau in [1, 0] + list(range(NT - 1, 2, -1)):
            if tau == 1:
                sj = None
            elif tau == 0:
                sj = NX - 1
            else:
                sj = tau - 3
            scan_step(tau, 1, sj)

        dump("Sst0", Sst[:, 0], TSst[0]); dump("Sst1", Sst[:, 1], TSst[1])
        if STOP <= 2:
            S.emit(nc, st)
            return nc
        rhsG = AR.alloc([4, 128], F32); TrhsG = T()
        rgh = [[AR.alloc([4, 128], BF16) for _ in range(2)] for _ in range(2)]
        rgl = [[AR.alloc([4, 128], BF16) for _ in range(2)] for _ in range(2)]
        Trgs = [[T(), T()], [T(), T()]]
        Dm = [AR.alloc([4, 128], F32) for _ in range(2)]; TDm = [T(), T()]
        abc = [AR.alloc([2, 128], F32) for _ in range(2)]; Tabc = [T(), T()]
        STm = [AR.alloc([4, 128], BF16) for _ in range(2)]; TSTm = [T(), T()]
        qtl = [AR.alloc([2, 2, 128], BF16) for _ in range(2)]; Tqtl = [T(), T()]
        qz = [AR.alloc([2, 2, 128], BF16) for _ in range(2)]; Tqz = [T(), T()]
        qaz = [AR.alloc([2, 4, 128], BF16) for _ in range(2)]; Tqaz = [T(), T()]
        memset("dve", qtl[0], 0.0, [Tqtl[0]]); memset("dve", qtl[1], 0.0, [Tqtl[1]])
        for p_ in range(2):
            memset("dve", qz[p_], 0.0, [Tqz[p_]]); memset("dve", qaz[p_], 0.0, [Tqaz[p_]])
        rden = AR.alloc([2, 4], F32); Trden = T()
        hsum = AR.alloc([4, 128], F32); Thsum = T()
        htmp = AR.alloc([4, 128], F32); Thtmp = T()
        hst = AR.alloc([16], F32); Thst = T()
        mltok = AR.alloc([512], BF16); Tmltok = T()
        mlT = AR.alloc([4, 128], BF16); TmlT = T()
        attT = AR.alloc([4, 128], BF16); TattT = T()
        PT = [AR.alloc([512], BF16) for _ in range(2)]; TPT = [T(), T()]
        zt = AR.alloc([512], F32); Tzt = T()
        xc = stg2; Txc = Tstg2
        x1t = [AR.alloc([1024], F32) for _ in range(2)]; Tx1t = [T(), T()]
        cst = AR.alloc([NX, 4], F32); Tcst = [T() for _ in range(NX)]
        Tx1d = [T() for _ in range(NX)]
        ptc = [0]
        if os.environ.get('MK_STATS'):
            print('ARENA end phase C alloc', AR.off)

        def c_prep(j):
            tau = j + 2; par = j % 2
            cols = slice(tau * 128, (tau + 1) * 128)
            qcols = slice(j * 128, (j + 1) * 128)
            for d in range(2):
                tri = tri_f if d == 0 else tri_b
                for h in range(4):
                    ts("dve", rhsG[:, h, :], tri, nlf[:, tau, d, h:h + 1], None, ALU.mult, None, [Tnlf, Tcf], [TrhsG])
                cp("dve", rgh[par][d], rhsG, [TrhsG], [Trgs[par][d]])
                tt("dve", rgl[par][d], rhsG, rgh[par][d], ALU.subtract, [TrhsG, Trgs[par][d]], [Trgs[par][d]])
            for hp in range(2):
                ps_ = slice(hp * 64, (hp + 1) * 64)
                cp("dve", qz[par][ps_, :, hp, :], mqk[ps_, 0:2, cols], [Tmqk[tau]], [Tqz[par]])
            for g in range(2):
                gs = slice(g * 64, (g + 1) * 64)
                cp("dve", qaz[par][gs, g], qT[gs, :, qcols], [TqT[j]], [Tqaz[par]])

        def c_gates(j):
            tau = j + 2; par = j % 2
            cols = slice(tau * 128, (tau + 1) * 128)
            for d in range(2):
                nstr = nstr_f if d == 0 else nstr_b
                madd4 = madd_f4 if d == 0 else madd_b4
                rh = rgh[par][d].rearrange("p h t -> p (h t)"); rl = rgl[par][d].rearrange("p h t -> p (h t)")
                mm(PB[1], nstr, rh, True, False, [Trgs[par][d], Tcb], [TB[1]])
                mm(PB[1], nstr, rl, False, False, [Trgs[par][d], Tcb], [TB[1]])
                mm(PB[1], identb, madd4.rearrange("p h t -> p (h t)"), False, True, [Tcb], [TB[1]])
                mm(PB[2], negones, rh, True, False, [Trgs[par][d], Tcb], [TB[2]])
                mm(PB[2], negones, rl, False, True, [Trgs[par][d], Tcb], [TB[2]])
                for h in range(4):
                    act(Dm[d][:, h, :], PB[1][:, h * 128:(h + 1) * 128], AF.Exp, [TB[1], Tgts[tau]], [TDm[d]],
                        bias=g5[:, tau, d, 0, h:h + 1])
                p2 = PB[2].rearrange("p (pr hh t) -> p pr hh t", pr=2, hh=2)
                act(abc[d][0:64], p2[0:64, :, 0, :], AF.Exp, [TB[2]], [Tabc[d]])
                act(abc[d][64:128], p2[64:128, :, 1, :], AF.Exp, [TB[2]], [Tabc[d]])
            for h in range(4):
                hp = h % 2; pr = h // 2
                mm(PB[3][:, h * 128:(h + 1) * 128], mqk[:, 2 + pr, cols], qz[par][:, pr, hp, :], True, True,
                   [Tmqk[tau], Tqz[par]], [TB[3]])
            for d in range(2):
                stt("dve", STm[d].rearrange("p h t -> p (h t)"), PB[3], 0.125, Dm[d].rearrange("p h t -> p (h t)"),
                    ALU.mult, ALU.mult, [TB[3], TDm[d]], [TSTm[d]])
                for hp in range(2):
                    ps_ = slice(hp * 64, (hp + 1) * 64)
                    stt("dve", qtl[d][ps_, :, hp, :], mqk[ps_, 0:2, cols], 0.125, abc[d][ps_], ALU.mult, ALU.mult,
                        [Tmqk[tau], Tabc[d]], [Tqtl[d]])

        def c_attn(j):
            tau = j + 2; par = j % 2
            for g in range(2):
                gs = slice(g * 64, (g + 1) * 64)
                blocks = []
                if j > 0:
                    blocks.append(("prev", kT[:, (j - 1) * 128:j * 128], Vst[:, tau - 1, :], [TkT[j - 1], TV[tau - 1]]))
                blocks.append(("cur", kT[:, j * 128:(j + 1) * 128], Vst[:, tau, :], [TkT[j], TV[tau]]))
                if j < NX - 1:
                    blocks.append(("next", kT[:, (j + 1) * 128:(j + 2) * 128], Vst[:, tau + 1, :], [TkT[j + 1], TV[tau + 1]]))
                for c in range(2):
                    blocks.append(("ctx", kcT[:, c * 128:(c + 1) * 128], Vst[:, c, :], [TkcT[c], TV[c]]))
                nb = len(blocks)
                for bi, (kind, kap, vap, Rk) in enumerate(blocks):
                    pb = ptc[0] % 2; ptc[0] += 1
                    msk = kind in ("prev", "next")
                    mm(PB[6], kap, qaz[par][:, g].rearrange("p i t -> p (i t)"), True, not msk, Rk + [Tqaz[par]], [TB[6]])
                    if msk:
                        mk_ = (mask_prev if kind == "prev" else mask_next).rearrange("p i t -> p (i t)")
                        mm(PB[6], identb, mk_, False, True, [Tcb], [TB[6]])
                    act(PT[pb], PB[6], AF.Exp, [TB[6]], [TPT[pb]], scale=0.125)
                    mm(PB[7], vap, PT[pb], bi == 0, bi == nb - 1, Rk + [TPT[pb]], [TB[7]])
                    mm(PB[3], onesb, PT[pb], bi == 0, bi == nb - 1, [Tcb, TPT[pb]], [TB[3]])
                z3 = zt.rearrange("p (i t) -> p i t", i=4)
                tt("dve", z3[gs], PB[3][gs].rearrange("p (i t) -> p i t", i=4), bc(esink[gs].unsqueeze(2), [64, 4, 128]),
                   ALU.add, [TB[3], Tes], [Tzt])
                recip(zt[gs], zt[gs], [Tzt], [Tzt])
                tt("dve", attT[gs].rearrange("p i t -> p (i t)"), PB[7][gs], zt[gs], ALU.mult, [TB[7], Tzt], [TattT])

        def c_num(j):
            tau = j + 2
            for d in range(2):
                for h in range(4):
                    hp = h % 2; pr = h // 2
                    o = PB[4 + pr][:, hp * 129:(hp + 1) * 129]
                    mm(o, STm[d][:, h, :], mVa[:, tau, h, :], True, False, [TSTm[d], TmVa[tau]], [TB[4 + pr]])
                    mm(o, qtl[d][:, pr, hp, :], Sst[:, d, j, pr, :], False, True, [Tqtl[d], TSst[d][j]], [TB[4 + pr]])
                for pr in range(2):
                    pv = PB[4 + pr][:, 0:258].rearrange("p (hh c) -> p hh c", hh=2)
                    ts("dve", rden[:, pr, 0:2], pv[:, :, 128], -1.0, 1.0, ALU.mult, ALU.max, [TB[4 + pr]], [Trden])
                    tt("dve", rden[:, pr, 0:2], pv[:, :, 128], rden[:, pr, 0:2], ALU.max, [TB[4 + pr], Trden], [Trden])
                    recip(rden[:, pr, 2:4], rden[:, pr, 0:2], [Trden], [Trden])
                    dst = (hsum if d == 0 else htmp)[:, 2 * pr:2 * pr + 2, :]
                    tt("dve", dst, pv[:, :, 0:128], bc(rden[:, pr, 2:4].unsqueeze(2), [128, 2, 128]), ALU.mult,
                       [TB[4 + pr], Trden], [Thsum if d == 0 else Thtmp])
            tt("dve", hsum, hsum, htmp, ALU.add, [Thsum, Thtmp], [Thsum])
            for h in range(4):
                act(htmp[:, h, :], hsum[:, h, :], AF.Square, [Thsum], [Thtmp, Thst], accum=hst[:, h:h + 1])
            act(hst[:, 8:12], hst[:, 0:4], AF.Ln, [Thst, Teps], [Thst], scale=1.0 / 128, bias=epsb)
            act(hst[:, 12:16], hst[:, 8:12], AF.Exp, [Thst], [Thst], scale=-0.5)
            tt("dve", htmp, hsum, bc(hst[:, 12:16].unsqueeze(2), [128, 4, 128]), ALU.mult, [Thsum, Thst], [Thtmp])
            tt("dve", mltok, htmp.rearrange("p h v -> p (h v)"), gsig[:, j, :], ALU.mult, [Thtmp, Tgsig[j]], [Tmltok])
            for i in range(4):
                tr(PB0h[:, i * 128:(i + 1) * 128], mltok[:, i * 128:(i + 1) * 128], identb, [Tmltok, Tcb], [TB[0]])
            cp("act", mlT.rearrange("p i t -> p (i t)"), PB0h[:, 0:512], [TB[0]], [TmlT])

        def c_out(j):
            b = j % 2
            dma(xc[b], x_d[j * 128:(j + 1) * 128, :], [], [Txc[b]])
            for hf in range(2):
                for kk in range(8):
                    lhs = attT[:, kk, :] if kk < 4 else mlT[:, kk - 4, :]
                    mm(PB[4 + hf], lhs, wout[:, kk, hf * 512:(hf + 1) * 512], kk == 0, kk == 7,
                       [TattT, TmlT, Twout[kk]], [TB[4 + hf]])
            sq = cst[:, j, :]
            for hf in range(2):
                act(x1t[b][:, hf * 512:(hf + 1) * 512], PB[4 + hf], AF.Square, [TB[4 + hf]], [Tx1t[b], Tcst[j]],
                    accum=sq[:, hf:hf + 1])
            tt("dve", sq[:, 2:3], sq[:, 0:1], sq[:, 1:2], ALU.add, [Tcst[j]], [Tcst[j]])
            rstd_ops(sq[:, 3:4], sq[:, 2:3], sq[:, 0:1], 1.0 / D, [Tcst[j]], [Tcst[j]], Tcst[j])
            for hf in range(2):
                cs = slice(hf * 512, (hf + 1) * 512)
                stt("dve", x1t[b][:, cs], PB[4 + hf], sq[:, 3:4], ggm[:, cs], ALU.mult, ALU.mult,
                    [TB[4 + hf], Tcst[j], Tggm], [Tx1t[b]])
            tt("dve", x1t[b], x1t[b], xc[b], ALU.add, [Tx1t[b], Txc[b]], [Tx1t[b]])
            dma(x1_d[j * 128:(j + 1) * 128, :], x1t[b], [Tx1t[b]], [Tx1d[j]])

        try:
            if STOP <= 3.5:
                raise StopBuild()
            c_prep(0)
            for j in range(NX):
                c_gates(j)
                if j + 1 < NX:
                    c_prep(j + 1)
                c_attn(j)
                c_num(j)
                c_out(j)
        except StopBuild:
            S.emit(nc, st)
            return nc
        S.barrier()
        AR.reset(m_persist)

        if STOP <= 3.9:
            S.emit(nc, st)
            return nc
        HB = 1024
        ynF = AR.alloc([8, HB], BF16); TynF = [T() for _ in range(8)]
        actT = AR.alloc([NCH, HB], BF16); TactT = [T() for _ in range(NCH)]
        wfo = AR.alloc([NCH, 1024], BF16); Twfo = [T() for _ in range(NCH)]
        wfs = [AR.alloc([2, 8, 128], F32) for _ in range(2)]; Twfs = [T(), T()]
        wfb = [AR.alloc([2, 8, 128], BF16) for _ in range(2)]; Twfb = [T(), T()]
        wos = [AR.alloc([1024], F32) for _ in range(2)]; Twos = [T(), T()]
        xd = [AR.alloc([1024], F32) for _ in range(2)]; Txd = [T(), T()]
        xnd = [AR.alloc([1024], BF16) for _ in range(2)]; Txnd = [T(), T()]
        junk2 = AR.alloc([1024], BF16); Tjunk2 = T()
        sgt = [AR.alloc([512], F32) for _ in range(2)]; Tsgt = [T(), T()]
        ot = [AR.alloc([1024], F32) for _ in range(2)]; Tot = [T(), T()]
        dst_ = AR.alloc([NX, 8], F32); Tdst = [T() for _ in range(NX)]
        Tout = T()
        wfi_v = wfi_d.rearrange("(k p) n -> p k n", p=128)

        for hb in range(2):
            for tl in range(8):
                j = hb * 8 + tl
                b = j % 2
                dma(xd[b], x1_d[j * 128:(j + 1) * 128, :], [Tx1d[j]], [Txd[b]])
                sq = dst_[:, j, :]
                act(junk2, xd[b], AF.Square, [Txd[b]], [Tjunk2, Tdst[j]], accum=sq[:, 0:1])
                rstd_ops(sq[:, 1:2], sq[:, 0:1], sq[:, 2:3], 1.0 / D, [Tdst[j]], [Tdst[j]], Tdst[j])
                ts("dve", xnd[b], xd[b], sq[:, 1:2], None, ALU.mult, None, [Txd[b], Tdst[j]], [Txnd[b]])
                for k in range(8):
                    tr(PB0h[:, k * 128:(k + 1) * 128], xnd[b][:, k * 128:(k + 1) * 128], identb, [Txnd[b], Tcb], [TB[0]])
                for k in range(8):
                    act(ynF[:, k, tl * 128:(tl + 1) * 128], PB0h[:, k * 128:(k + 1) * 128], AF.Identity,
                        [TB[0], TGf, TmodT], [TynF[tl]], scale=Gf[:, k, 0:1], bias=modT[:, 24 + k, 0:1])
            for c in range(NCH):
                wb = c % 2
                dma(wfs[wb][:, 0], wfi_v[:, :, c * 128:(c + 1) * 128], [], [Twfs[wb]])
                dma(wfs[wb][:, 1], wfi_v[:, :, FH + c * 128:FH + (c + 1) * 128], [], [Twfs[wb]])
                cp("act", wfb[wb], wfs[wb], [Twfs[wb]], [Twfb[wb]])
                if hb == 0:
                    dma(wos[wb], wfo_d[c * 128:(c + 1) * 128, :], [], [Twos[wb]])
                    cp("act", wfo[:, c, :], wos[wb], [Twos[wb]], [Twfo[c]])
                for tb in range(2):
                    pg = PB[1 + 2 * tb]; pu = PB[2 + 2 * tb]
                    tcs = slice(tb * 512, (tb + 1) * 512)
                    for k in range(8):
                        mm(pg, wfb[wb][:, 0, k, :], ynF[:, k, tcs], k == 0, k == 7, [Twfb[wb]] + TynF[tb * 4:(tb + 1) * 4], [TB[1 + 2 * tb]])
                    for k in range(8):
                        mm(pu, wfb[wb][:, 1, k, :], ynF[:, k, tcs], k == 0, k == 7, [Twfb[wb]] + TynF[tb * 4:(tb + 1) * 4], [TB[2 + 2 * tb]])
                    act(sgt[tb], pg, AF.Silu, [TB[1 + 2 * tb]], [Tsgt[tb]])
                    tt("dve", actT[:, c, tcs], pu, sgt[tb], ALU.mult, [TB[2 + 2 * tb], Tsgt[tb]], [TactT[c]])
            for tl in range(8):
                j = hb * 8 + tl
                b = j % 2
                dma(xd[b], x1_d[j * 128:(j + 1) * 128, :], [Tx1d[j]], [Txd[b]])
                for hf in range(2):
                    for c in range(NCH):
                        mm(PB[5 + hf], actT[:, c, tl * 128:(tl + 1) * 128], wfo[:, c, hf * 512:(hf + 1) * 512],
                           c == 0, c == NCH - 1, [TactT[c], Twfo[c]], [TB[5 + hf]])
                sq = dst_[:, j, :]
                for hf in range(2):
                    act(ot[b][:, hf * 512:(hf + 1) * 512], PB[5 + hf], AF.Square, [TB[5 + hf]], [Tot[b], Tdst[j]],
                        accum=sq[:, 3 + hf:4 + hf])
                tt("dve", sq[:, 5:6], sq[:, 3:4], sq[:, 4:5], ALU.add, [Tdst[j]], [Tdst[j]])
                rstd_ops(sq[:, 6:7], sq[:, 5:6], sq[:, 7:8], 1.0 / D, [Tdst[j]], [Tdst[j]], Tdst[j])
                for hf in range(2):
                    cs = slice(hf * 512, (hf + 1) * 512)
                    stt("dve", ot[b][:, cs], PB[5 + hf], sq[:, 6:7], ggf[:, cs], ALU.mult, ALU.mult,
                        [TB[5 + hf], Tdst[j], Tggf], [Tot[b]])
                tt("dve", ot[b], ot[b], xd[b], ALU.add, [Tot[b], Txd[b]], [Tot[b]])
                dma(out_d[j * 128:(j + 1) * 128, :], ot[b], [Tot[b]], [Tout, Tx1d[j]])

        S.emit(nc, st)
    return nc


def _consts():
    idx = np.arange(128)
    u = idx[:, None]; v = idx[None, :]
    cf = np.zeros((128, 2, 128), np.float32)
    cf[:, 0] = np.where(u <= v, 1.0, 0.0)
    cf[:, 1] = np.where(u >= v, 1.0, 0.0)
    cb = np.zeros((128, 21, 128), np.float32)
    cb[:, 0] = np.eye(128)
    cb[:, 1] = 1.0
    mp = np.where(u >= v, 0.0, NEG)
    mn = np.where(u <= v, 0.0, NEG)
    for i in range(4):
        cb[:, 2 + i] = mp
        cb[:, 6 + i] = mn
        cb[:, 13 + i] = np.where(u <= v, 0.0, NEG)
        cb[:, 17 + i] = np.where(u >= v, 0.0, NEG)
    cb[:, 10] = np.where(u > v, -1.0, 0.0)
    cb[:, 11] = np.where(u < v, -1.0, 0.0)
    cb[:, 12] = -1.0
    cb = cb.astype(ml_dtypes.bfloat16)
    half = 16
    inv_freq = np.power(np.float32(10000.0), -np.arange(half, dtype=np.float32) / np.float32(half)).astype(np.float32)
    t = np.arange(S_LEN)
    row = (t // 64).astype(np.float32); col = (t % 64).astype(np.float32)
    ang_r = row[:, None] * inv_freq; ang_c = col[:, None] * inv_freq
    cos64 = np.concatenate([np.cos(ang_r), np.cos(ang_r), np.cos(ang_c), np.cos(ang_c)], axis=1)
    sin64 = np.concatenate([np.sin(ang_r), np.sin(ang_r), np.sin(ang_c), np.sin(ang_c)], axis=1)
    rope = np.stack([cos64, sin64], axis=1).astype(np.float32)
    rope = rope.reshape(NX, 128, 2, 64).transpose(1, 0, 2, 3).copy()
    return cf, cb, rope


_CACHE = {}


def kernel(x, c, ctx, c_ctx, w_ada, b_ada, g_pre_mix, w_in, w_conv_qk, b_gates, attn_sink,
           g_mlstm_out, w_out, g_post_mix, g_pre_ffn, w_ffn_in, w_ffn_out, g_post_ffn):
    f = lambda a: np.ascontiguousarray(np.asarray(a, dtype=np.float32))
    x = f(x); c = f(c); ctx = f(ctx); c_ctx = f(c_ctx)
    w_ada = f(w_ada)[0]; b_ada = f(b_ada)[0]; w_in = f(w_in)[0]; w_out = f(w_out)[0]
    w_ffn_in = f(w_ffn_in)[0]; w_ffn_out = f(w_ffn_out)[0]
    perm = []
    for i in range(4):
        for g in range(2):
            h = g * 4 + i
            perm += list(range(h * 64, (h + 1) * 64))
    perm += list(range(512, 640))
    perm += list(range(640, 768))
    perm += list(range(2304, 2320))
    perm += list(range(1280, 1792))
    perm += list(range(1792, 2304))
    perm += list(range(768, 1280))
    w_in_p = np.ascontiguousarray(w_in[:, perm])
    rperm = []
    for i in range(4):
        for g in range(2):
            h = g * 4 + i
            rperm += list(range(h * 64, (h + 1) * 64))
    rperm += list(range(512, 1024))
    w_out_p = np.ascontiguousarray(w_out[rperm, :])
    vfm = np.zeros((128, 80), np.float32)
    vfm[:, 32:80] = b_ada.reshape(48, 128).T
    vfm[:, 0:8] = f(g_pre_mix)[0].reshape(8, 128).T
    vfm[:, 8:16] = f(g_pre_ffn)[0].reshape(8, 128).T
    wc = f(w_conv_qk)[0]
    vfm[:, 16:28] = wc.reshape(3, 4, 128).transpose(2, 1, 0).reshape(128, 12)
    sk = f(attn_sink)[0]
    vfm[0:64, 28:32] = sk[0:4][None, :]
    vfm[64:128, 28:32] = sk[4:8][None, :]
    vrow = np.concatenate([f(g_post_mix)[0], f(g_post_ffn)[0], f(g_mlstm_out)[0], f(b_gates)[0]])[None, :]
    vrow = np.ascontiguousarray(vrow)
    cf, cb, rope = _consts()
    if "nc" not in _CACHE:
        _CACHE["nc"] = build_program()
    nc = _CACHE["nc"]
    in_maps = []
    tag = lambda w, b: np.concatenate([w, np.full((1, w.shape[1]), float(b), np.float32)], axis=0)
    for b in range(8):
        ccb = np.stack([c[b].reshape(8, 128).T, c_ctx.reshape(8, 128).T], axis=2).reshape(128, 16)
        in_maps.append({
            "x": x[b], "ctx": ctx[b], "cc": np.ascontiguousarray(ccb),
            "w_ada": tag(w_ada, b), "b_ada": b_ada[None, :], "w_in": tag(w_in_p, b), "w_out": tag(w_out_p, b),
            "w_ffn_in": tag(w_ffn_in, b), "w_ffn_out": tag(w_ffn_out, b), "vec_fm": vfm, "vec_row": vrow,
            "cst_f32": cf, "cst_bf16": cb,
            "rope": np.concatenate([rope.reshape(128, NX * 128), np.full((128, 1), float(b), np.float32)], axis=1),
        })
    res = run_bass_kernel_spmd(nc, in_maps, core_ids=list(range(8)))
    out = np.stack([np.asarray(r["out"], dtype=np.float32) for r in res.results], axis=0)
    if DEBUG:
        kernel.dbg = [dict(r) for r in res.results]
    return out
```
